# Optimizing a Trainium2 kernel written in Bass

```python
import math
import jax, jax.numpy as jnp
from jax import lax
import numpy as np

D_MODEL = 2048
BATCH = 4
SEQ = 2048
DEPTH = 1

N_META = 16
S5_WIDTH = 1024
S5_GROUP = 16
S5_GROUPS = S5_WIDTH // S5_GROUP
S5_STATE = 64
N_DIR = 2
CONV_WIDTH = 1024
CONV_K = 3
FFN_HIDDEN = ((math.ceil(8 * D_MODEL / 3) + 255) // 256) * 256
IN_COLS = S5_WIDTH + 3 * CONV_WIDTH + 2 * D_MODEL
RMS_EPS = 1e-6
DT_MIN = 1e-3
DT_MAX = 1e-1
LAM_RE_MAX = -1e-4

kernel_name = 'hybrid_s5_shortconv_gated_encoder_block'


def rms_norm(x, g):
    xf = x.astype(jnp.float32)
    r = lax.rsqrt(jnp.mean(xf * xf, axis=-1, keepdims=True) + RMS_EPS)
    return (xf * r * g.astype(jnp.float32)).astype(x.dtype)


def _complex_linear_combine(e1, e2):
    a1r, a1i, b1r, b1i = e1
    a2r, a2i, b2r, b2i = e2
    ar = a2r * a1r - a2i * a1i
    ai = a2r * a1i + a2i * a1r
    br = a2r * b1r - a2i * b1i + b2r
    bi = a2r * b1i + a2i * b1r + b2i
    return (ar, ai, br, bi)


def s5_bidirectional(u, lam_re, lam_im, log_dt, b_re, b_im, c_re, c_im, d_skip):
    bsz, seq_len, _ = u.shape
    f32 = jnp.float32
    uf = u.astype(f32)
    ug = uf.reshape(bsz, seq_len, S5_GROUPS, S5_GROUP)
    y = uf * d_skip.astype(f32)
    for direction in range(N_DIR):
        lr = jnp.minimum(lam_re[direction].astype(f32), LAM_RE_MAX)
        li = lam_im[direction].astype(f32)
        dt = jnp.exp(log_dt[direction].astype(f32))[:, None]
        mag = jnp.exp(lr * dt)
        ar = mag * jnp.cos(li * dt)
        ai = mag * jnp.sin(li * dt)
        den = lr * lr + li * li
        nr = ar - 1.0
        zr = (nr * lr + ai * li) / den
        zi = (ai * lr - nr * li) / den
        bu_r = jnp.einsum('blgc,gpc->blgp', ug, b_re[direction].astype(f32))
        bu_i = jnp.einsum('blgc,gpc->blgp', ug, b_im[direction].astype(f32))
        xr = zr * bu_r - zi * bu_i
        xi = zr * bu_i + zi * bu_r
        a_r = jnp.broadcast_to(ar, xr.shape)
        a_i = jnp.broadcast_to(ai, xr.shape)
        _, _, sr, si = lax.associative_scan(_complex_linear_combine, (a_r, a_i, xr, xi),
                                            axis=1, reverse=(direction == 1))
        yd = (jnp.einsum('blgp,gcp->blgc', sr, c_re[direction].astype(f32))
              - jnp.einsum('blgp,gcp->blgc', si, c_im[direction].astype(f32)))
        y = y + yd.reshape(bsz, seq_len, S5_WIDTH)
    return y.astype(u.dtype)


def centred_dwconv3(v, w, b):
    seq_len = v.shape[1]
    vp = jnp.pad(v, ((0, 0), (1, 1), (0, 0)))
    return w[0] * vp[:, :seq_len] + w[1] * vp[:, 1:seq_len + 1] + w[2] * vp[:, 2:] + b


def setup_inputs(seed: int = 0) -> dict:
    key = jax.random.key(seed)
    ks = jax.random.split(key, 32)
    f32 = jnp.float32
    G, P, Q = S5_GROUPS, S5_STATE, S5_GROUP

    def nrm(k, shape, scale):
        return jax.random.normal(k, shape, f32) * scale

    x = nrm(ks[0], (BATCH, SEQ, D_MODEL), 1.0)
    meta = nrm(ks[1], (N_META, D_MODEL), 1.0)
    g_mix_pre = 1.0 + nrm(ks[2], (DEPTH, D_MODEL), 0.02)
    g_mix_post = 1.0 + nrm(ks[3], (DEPTH, D_MODEL), 0.02)
    g_ffn_pre = 1.0 + nrm(ks[4], (DEPTH, D_MODEL), 0.02)
    g_ffn_post = 1.0 + nrm(ks[5], (DEPTH, D_MODEL), 0.02)
    w_in = nrm(ks[6], (DEPTH, D_MODEL, IN_COLS), D_MODEL ** -0.5)
    gate_b = nrm(ks[7], (DEPTH, 2 * D_MODEL), 0.01)
    n_idx = jnp.arange(P, dtype=f32)
    lam_re = -0.5 + nrm(ks[8], (DEPTH, N_DIR, G, P), 0.01)
    lam_im = math.pi * n_idx + nrm(ks[9], (DEPTH, N_DIR, G, P), 0.01)
    log_dt = jax.random.uniform(ks[10], (DEPTH, N_DIR, G), f32,
                                math.log(DT_MIN), math.log(DT_MAX))
    b_re = nrm(ks[11], (DEPTH, N_DIR, G, P, Q), (2.0 * Q) ** -0.5)
    b_im = nrm(ks[12], (DEPTH, N_DIR, G, P, Q), (2.0 * Q) ** -0.5)
    c_re = nrm(ks[13], (DEPTH, N_DIR, G, Q, P), (2.0 * P) ** -0.5)
    c_im = nrm(ks[14], (DEPTH, N_DIR, G, Q, P), (2.0 * P) ** -0.5)
    d_skip = nrm(ks[15], (DEPTH, S5_WIDTH), 1.0)
    w_glu = nrm(ks[16], (DEPTH, S5_WIDTH, S5_WIDTH), S5_WIDTH ** -0.5)
    b_glu = nrm(ks[17], (DEPTH, S5_WIDTH), 0.01)
    w_s_up = nrm(ks[18], (DEPTH, S5_WIDTH, D_MODEL), S5_WIDTH ** -0.5)
    conv_w = nrm(ks[19], (DEPTH, CONV_K, CONV_WIDTH), CONV_K ** -0.5)
    conv_b = nrm(ks[20], (DEPTH, CONV_WIDTH), 0.01)
    w_c_up = nrm(ks[21], (DEPTH, CONV_WIDTH, D_MODEL), CONV_WIDTH ** -0.5)
    w_o = nrm(ks[22], (DEPTH, D_MODEL, D_MODEL), D_MODEL ** -0.5)
    w_ffn_in = nrm(ks[23], (DEPTH, D_MODEL, 2 * FFN_HIDDEN), D_MODEL ** -0.5)
    w_ffn_out = nrm(ks[24], (DEPTH, FFN_HIDDEN, D_MODEL), FFN_HIDDEN ** -0.5)
    return {'x': x, 'meta': meta, 'g_mix_pre': g_mix_pre, 'g_mix_post': g_mix_post,
            'g_ffn_pre': g_ffn_pre, 'g_ffn_post': g_ffn_post, 'w_in': w_in, 'gate_b': gate_b,
            'lam_re': lam_re, 'lam_im': lam_im, 'log_dt': log_dt, 'b_re': b_re, 'b_im': b_im,
            'c_re': c_re, 'c_im': c_im, 'd_skip': d_skip, 'w_glu': w_glu, 'b_glu': b_glu,
            'w_s_up': w_s_up, 'conv_w': conv_w, 'conv_b': conv_b, 'w_c_up': w_c_up,
            'w_o': w_o, 'w_ffn_in': w_ffn_in, 'w_ffn_out': w_ffn_out}


def reference(x, meta, g_mix_pre, g_mix_post, g_ffn_pre, g_ffn_post, w_in, gate_b,
              lam_re, lam_im, log_dt, b_re, b_im, c_re, c_im, d_skip, w_glu, b_glu,
              w_s_up, conv_w, conv_b, w_c_up, w_o, w_ffn_in, w_ffn_out):
    bsz = x.shape[0]
    meta_b = jnp.broadcast_to(meta[None].astype(x.dtype), (bsz, N_META, D_MODEL))
    h_stream = jnp.concatenate([meta_b, x], axis=1)
    splits = [S5_WIDTH, S5_WIDTH + CONV_WIDTH, S5_WIDTH + 2 * CONV_WIDTH,
              S5_WIDTH + 3 * CONV_WIDTH, S5_WIDTH + 3 * CONV_WIDTH + D_MODEL]
    for l in range(DEPTH):
        h = rms_norm(h_stream, g_mix_pre[l])
        proj = h @ w_in[l]
        u_s, x_c, b_c, c_c, gl_s, gl_c = jnp.split(proj, splits, axis=-1)
        gb_s, gb_c = jnp.split(gate_b[l], 2)
        y_s = s5_bidirectional(u_s, lam_re[l], lam_im[l], log_dt[l], b_re[l], b_im[l],
                               c_re[l], c_im[l], d_skip[l])
        y_s = jax.nn.gelu(y_s)
        y_s = y_s * jax.nn.sigmoid(y_s @ w_glu[l] + b_glu[l])
        y_s = y_s @ w_s_up[l]
        v = centred_dwconv3(c_c * x_c, conv_w[l], conv_b[l])
        y_c = (b_c * v) @ w_c_up[l]
        merged = jax.nn.sigmoid(gl_s + gb_s) * y_s + jax.nn.sigmoid(gl_c + gb_c) * y_c
        h_stream = h_stream + rms_norm(merged @ w_o[l], g_mix_post[l])
        h = rms_norm(h_stream, g_ffn_pre[l])
        gate, up = jnp.split(h @ w_ffn_in[l], 2, axis=-1)
        f = (jax.nn.silu(gate) * up) @ w_ffn_out[l]
        h_stream = h_stream + rms_norm(f, g_ffn_post[l])
    return h_stream[:, N_META:]
```

```python
import math
from contextlib import ExitStack

import numpy as np
import ml_dtypes

import concourse.bass as bass
import concourse.mybir as mybir
from concourse.ap import AP
from concourse.bass_utils import run_bass_kernel_spmd

F32 = mybir.dt.float32
BF16 = mybir.dt.bfloat16
ALU = mybir.AluOpType
AF = mybir.ActivationFunctionType

D = 2048
NTOK = 1024
NSEQ = 2304
NCH = NSEQ // 8
FFN = 5632
TWO_PI = 2.0 * math.pi
DEBUG = False
STOP = None
NCORES = 8


class _Stop(Exception):
    pass


class Sched:
    def __init__(self, nc, stack, n_dma_sems=72):
        self.nc = nc
        self.eng = {"pe": nc.tensor, "act": nc.scalar, "dve": nc.vector,
                    "pool": nc.gpsimd, "sp": nc.sync}
        self.sem = {k: stack.enter_context(nc.semaphore("s_" + k)) for k in self.eng}
        self.cnt = {k: 0 for k in self.eng}
        self.known = {k: {} for k in self.eng}
        self.free_dma = [stack.enter_context(nc.semaphore("d%d" % i)) for i in range(n_dma_sems)]
        self.dma_sem = {}
        self.dma_cnt = {}
        self.last_w = {}
        self.readers = {}
        self.inherit = {}
        self.out_events = []

    def _wait(self, e, ev):
        src, val = ev
        if src == e and e == "pe":
            return
        kn = self.known[e]
        if kn.get(src, 0) >= val:
            return
        if not isinstance(src, str):
            val = self.dma_cnt[src[1]]
        kn[src] = val
        sem = self.sem[src] if isinstance(src, str) else self.dma_sem[src[1]]
        self.eng[e].wait_ge(sem, val)

    def _deps(self, e, reads, writes):
        evs = []
        for k in reads:
            w = self.last_w.get(k)
            if w is not None:
                evs.append(w)
            if isinstance(k, tuple) and k[0] == "ps":
                for s_, v_ in self.readers.get(k, {}).items():
                    if s_ != e:
                        evs.append((s_, v_))
        for k in writes:
            w = self.last_w.get(k)
            if w is not None:
                evs.append(w)
            else:
                tn = k[0] if isinstance(k, tuple) else k
                evs.extend(self.inherit.get(tn, []))
            evs.extend(self.readers.get(k, {}).items())
        for ev in evs:
            self._wait(e, ev)

    def _record(self, ev, reads, writes):
        for k in reads:
            r = self.readers.setdefault(k, {})
            if r.get(ev[0], 0) < ev[1]:
                r[ev[0]] = ev[1]
        for k in writes:
            self.last_w[k] = ev
            self.readers[k] = {}

    def op(self, e, fn, reads=(), writes=()):
        self._deps(e, reads, writes)
        ins = fn(self.eng[e])
        self.cnt[e] += 1
        ins.then_inc(self.sem[e], 1)
        ev = (e, self.cnt[e])
        self._record(ev, reads, writes)
        return ev

    def dma(self, e, out, in_, slot, reads=(), writes=(), is_output=False):
        self._deps(e, reads, writes)
        if slot not in self.dma_sem:
            self.dma_sem[slot] = self.free_dma.pop()
            self.dma_cnt[slot] = 0
        self.dma_cnt[slot] += 16
        self.eng[e].dma_start(out=out, in_=in_).then_inc(self.dma_sem[slot], 16)
        ev = (("dma", slot), self.dma_cnt[slot])
        self._record(ev, reads, writes)
        if is_output:
            self.out_events.append(ev)
        return ev

    def all_events(self, tnames):
        evs = {}
        for table in (self.last_w,):
            for k, ev in table.items():
                tn = k[0] if isinstance(k, tuple) else k
                if tn in tnames and evs.get(ev[0], 0) < ev[1]:
                    evs[ev[0]] = ev[1]
        for k, r in self.readers.items():
            tn = k[0] if isinstance(k, tuple) else k
            if tn in tnames:
                for s, v in r.items():
                    if evs.get(s, 0) < v:
                        evs[s] = v
        for tn in tnames:
            for s, v in self.inherit.get(tn, []):
                if evs.get(s, 0) < v:
                    evs[s] = v
        return list(evs.items())

    def finish(self, e="sp"):
        for ev in self.out_events:
            self._wait(e, ev)


class Arena:
    LO = 16512
    HI = 229376

    def __init__(self, nc, S):
        self.nc, self.S = nc, S
        self.top = self.LO
        self.live = []
        self.dead = []
        self.uid = 0
        self.addr = {}
        self.peak = 0

    def overlay(self, name, shape, dtype, base, off=0):
        self.uid += 1
        t = self.nc.alloc_sbuf_tensor_at("%s_%d" % (name, self.uid), list(shape), dtype, offset=self.addr[base] + off)
        self.S.inherit[name] = self.S.all_events({base})
        return t

    def alloc(self, name, shape, dtype):
        nbytes = int(np.prod(shape[1:])) * (2 if dtype == BF16 else 4)
        lo = (self.top + 63) // 64 * 64
        hi = lo + nbytes
        assert hi <= self.HI, ("SBUF overflow", name, hi)
        assert getattr(self, "fixed_lo", None) is None or hi <= self.fixed_lo, ("stack runs into fixed slot", name, hi)
        self.top = hi
        self.uid += 1
        t = self.nc.alloc_sbuf_tensor_at("%s_%d" % (name, self.uid), list(shape), dtype, offset=lo)
        self.live.append((name, lo, hi))
        self.addr[name] = lo
        self.peak = max(self.peak, hi)
        evs = []
        for (dlo, dhi, devs) in self.dead:
            if dlo < hi and lo < dhi:
                evs.extend(devs)
        if evs:
            self.S.inherit[name] = evs
        return t

    def fixed(self, name, shape, dtype, lo):
        nbytes = int(np.prod(shape[1:])) * (2 if dtype == BF16 else 4)
        hi = lo + nbytes
        assert hi <= self.HI and lo >= self.top, ("fixed slot collides with the stack", name, lo, self.top)
        self.uid += 1
        t = self.nc.alloc_sbuf_tensor_at("%s_%d" % (name, self.uid), list(shape), dtype, offset=lo)
        evs = []
        for (dlo, dhi, devs) in self.dead:
            if dlo < hi and lo < dhi:
                evs.extend(devs)
        self.S.inherit[name] = evs
        self.fixed_lo = lo
        return t

    def retire_fixed(self, names, lo, hi):
        self.dead.append((lo, hi, self.S.all_events(set(names))))
        self.fixed_lo = None

    def mark(self):
        return (self.top, len(self.live))

    def release(self, mark):
        top, n = mark
        gone = self.live[n:]
        self.live = self.live[:n]
        if gone:
            names = set(g[0] for g in gone)
            evs = self.S.all_events(names)
            lo = min(g[1] for g in gone)
            hi = max(g[2] for g in gone)
            keep = []
            for dd in self.dead:
                if dd[0] >= lo and dd[1] <= hi:
                    evs = evs + dd[2]
                else:
                    keep.append(dd)
            self.dead = keep + [(lo, hi, evs)]
        self.top = top


def mk(v, dims, off=0):
    return AP(v.tensor, v.offset + off, [list(v.ap[0])] + [list(d) for d in dims])


def build_nc(debug=False):
    nc = bass.Bass("TRN2", target_bir_lowering=False)

    def din(name, shape, dt=F32):
        return nc.dram_tensor(name, list(shape), dt, kind="ExternalInput").ap()

    xseq = din("xseq", [NSEQ, D])
    pA_d = din("pA", [128, 4288])
    vecs_d = din("vecs", [128, 128])
    gpost_d = din("gpost", [128, 2 * D])
    cF_d = din("cF", [128, 768])
    strips_d = din("strips", [128, 1920], BF16)
    w_in = din("w_in", [D, 8192])
    w_glu = din("w_glu", [1024, 1024])
    w_s_up = din("w_s_up", [1024, D])
    w_c_up = din("w_c_up", [1024, D])
    w_o = din("w_o", [D, D])
    w_fi = din("w_ffn_in", [D, 2 * FFN])
    w_fo = din("w_ffn_out", [FFN, D])
    out_d = nc.dram_tensor("out", [NTOK, D], F32, kind="ExternalOutput").ap()
    hs_scr = None
    dbg = {}

    w_in_v = w_in.rearrange("(k p) n -> p k n", p=128)
    w_glu_v = w_glu.rearrange("(k p) n -> p k n", p=128)
    w_s_v = w_s_up.rearrange("(k p) n -> p k n", p=128)
    w_c_v = w_c_up.rearrange("(k p) n -> p k n", p=128)
    w_o_v = w_o.rearrange("(k p) n -> p k n", p=128)
    w_fi_v = w_fi.rearrange("(k p) n -> p k n", p=128)
    w_fo_v = w_fo.rearrange("(k p) n -> p k n", p=128)

    with ExitStack() as stack:
        S = Sched(nc, stack)
        try:
            _program(nc, S, locals())
        except _Stop:
            pass
        S.finish("sp")
    return nc, dbg


def _program(nc, S, env):
    globals_ = env
    xseq, pA_d, vecs_d, gpost_d, cF_d, strips_d = (env[k] for k in ("xseq", "pA_d", "vecs_d", "gpost_d", "cF_d", "strips_d"))
    w_in_v, w_glu_v, w_s_v, w_c_v, w_o_v, w_fi_v, w_fo_v = (env[k] for k in ("w_in_v", "w_glu_v", "w_s_v", "w_c_v", "w_o_v", "w_fi_v", "w_fo_v"))
    out_d, hs_scr, dbg, debug = env["out_d"], env["hs_scr"], env["dbg"], env["debug"]

    pend = [False]

    def checkpoint(name):
        if STOP == name:
            if name in ("A0", "C0", "E0"):
                pend[0] = True
            else:
                raise _Stop()

    if True:
        A = Arena(nc, S)
        ps = [nc.alloc_psum_tensor("ps%d" % i, [128, 512], F32) for i in range(8)]
        ps_rr = [0]

        ps_pool = [list(range(8))]

        def bank():
            pool = ps_pool[0]
            i = pool[ps_rr[0] % len(pool)]
            ps_rr[0] += 1
            return ps[i], ("ps", i)

        def dump(name, src_ap, shape, key, dt=F32):
            if not debug:
                return
            d = nc.dram_tensor("dbg_" + name, list(shape), dt, kind="ExternalOutput").ap()
            dbg[name] = d
            S.dma("sp", d, src_ap, "dbg_" + name, reads=[key], is_output=True)
            if pend[0]:
                raise _Stop()

        cF = A.alloc("cF", [128, 768], F32)
        strips = A.alloc("strips", [128, 8, 240], BF16)
        vecs = A.alloc("vecs", [128, 128], F32)
        S.dma("sp", cF[:], cF_d, "c0", writes=["cF"])
        S.dma("sp", strips[:].rearrange("p a b -> p (a b)"), strips_d, "c1", writes=["strips"])
        S.dma("sp", vecs[:], vecs_d, "c2", writes=["vecs"])
        ident = cF[:, 0:128]
        maskA = cF[:, 128:256]
        maskB = cF[:, 256:384]
        I2 = cF[:, 384:448]
        mvals = cF[:, 448:465]
        kpos = cF[:, 480:768]
        V_GPRE, V_GFPRE, V_GB, V_DSK, V_BGLU, V_CW, V_CB = 0, 16, 32, 64, 72, 80, 104

        hT = A.alloc("hT", [128, 16, 1026], BF16)
        small = A.alloc("small", [128, 64], F32)
        m_main = A.mark()

        gT = A.alloc("gT", [128, 8, NTOK], BF16)
        m1 = A.mark()
        u_fm = A.alloc("u_fm", [128, 8, NSEQ], BF16)
        m2 = A.mark()
        xin = [(A.alloc("xin%d" % i, [128, D], F32), ("xin%d" % i,)) for i in range(3)]
        sq = (A.alloc("sq", [128, D], BF16), ("sq",))
        hTts = [A.alloc("hTt%d" % i, [128, 16, 512], BF16) for i in range(2)]
        wus = A.alloc("wus", [128, 16, 1024], BF16)

        for kq in range(4):
            S.dma("pool", wus[:, 4 * kq:4 * kq + 4, :], w_in_v[:, 4 * kq:4 * kq + 4, 0:1024],
                  "wus", writes=[("wus", kq)])

        tile_ctr = [0]

        def norm_tile(row0, dst, dst_col, gcol, src_d=xseq, src_sb=None, dst_key=None, bufs=None, sqb=None):
            i = tile_ctr[0]
            tile_ctr[0] += 1
            bufs = bufs or xin
            sqt, sqk = sqb or sq
            xb, xk = bufs[i % 2]
            sc = small[:, (i % 8) * 2:(i % 8) * 2 + 1]
            sk = ("small", i % 8)
            if src_sb is None:
                S.dma("sp", xb[:], src_d[row0:row0 + 128, :], xk[0], writes=[xk])
                src, srck = xb[:], xk
            else:
                src, srck = src_sb
            S.op("act", lambda e: e.activation(out=sqt[:], in_=src, func=AF.Square, accum_out=sc),
                 reads=[srck], writes=[sqk, sk])
            S.op("act", lambda e: e.activation(out=sc, in_=sc, func=AF.Sqrt, bias=1e-6, scale=1.0 / D),
                 reads=[sk], writes=[sk])
            S.op("dve", lambda e: e.reciprocal(out=sc, in_=sc), reads=[sk], writes=[sk])
            S.op("act", lambda e: e.activation(out=xb[:], in_=src, func=AF.Copy, scale=sc),
                 reads=[srck, sk], writes=[xk])
            for q in range(4):
                pb, pk = bank()
                for kk in range(4):
                    k = 4 * q + kk
                    S.op("pe", lambda e, kk=kk, k=k: e.transpose(
                        out=pb[:, kk * 128:(kk + 1) * 128], in_=xb[:, k * 128:(k + 1) * 128], identity=ident),
                        reads=[xk, "cF"], writes=[pk])
                gv = vecs[:, gcol + 4 * q:gcol + 4 * q + 4]
                g_b = mk(gv, [[1, 4], [0, 128]])
                eng = "dve" if q % 2 == 0 else "pool"
                eng = "dve"
                S.op(eng, lambda e, q=q, g_b=g_b, pb=pb: e.tensor_tensor(
                    out=dst[:, 4 * q:4 * q + 4, dst_col:dst_col + 128],
                    in0=pb[:].rearrange("p (a b) -> p a b", a=4), in1=g_b, op=ALU.mult),
                    reads=[pk, "vecs"], writes=[dst_key])

        def usproj(src, src_key, ncols, ucol0):
            for m in range(8):
                pb, pk = bank()
                for k in range(16):
                    S.op("pe", lambda e, k=k, m=m: e.matmul(
                        pb[:, 0:ncols], lhsT=wus[:, k, m * 128:(m + 1) * 128], rhs=src[:, k, 0:ncols],
                        start=(k == 0), stop=(k == 15)),
                        reads=[("wus", k // 4), src_key], writes=[pk])
                S.op("act", lambda e, m=m, pb=pb: e.copy(
                    out=mk(u_fm[:, m, :], [[NCH, 8], [1, ncols // 8]], off=ucol0 // 8),
                    in_=mk(pb[:, 0:ncols], [[1, 8], [8, ncols // 8]])),
                     reads=[pk], writes=[("u_fm", m)])

        def norm_stats(i, row0):
            xb, xk = xin[i % 3]
            sqt, sqk = sq
            sc = small[:, (i % 8) * 2:(i % 8) * 2 + 1]
            sk = ("small", i % 8)
            S.dma("sp", xb[:], xseq[row0:row0 + 128, :], xk[0], writes=[xk])
            S.op("act", lambda e: e.activation(out=sqt[:], in_=xb[:], func=AF.Square, accum_out=sc),
                 reads=[xk], writes=[sqk, sk])
            S.op("act", lambda e: e.activation(out=sc, in_=sc, func=AF.Sqrt, bias=1e-6, scale=1.0 / D),
                 reads=[sk], writes=[sk])
            S.op("dve", lambda e: e.reciprocal(out=sc, in_=sc), reads=[sk], writes=[sk])
            S.op("act", lambda e: e.activation(out=xb[:], in_=xb[:], func=AF.Copy, scale=sc),
                 reads=[xk, sk], writes=[xk])

        def norm_trans(i, dst, dst_col, gcol, dst_key):
            xb, xk = xin[i % 3]
            for q in range(4):
                pb, pk = bank()
                for kk in range(4):
                    k = 4 * q + kk
                    S.op("pe", lambda e: e.transpose(
                        out=pb[:, kk * 128:(kk + 1) * 128], in_=xb[:, k * 128:(k + 1) * 128], identity=ident),
                        reads=[xk, "cF"], writes=[pk])
                g_b = mk(vecs[:, gcol + 4 * q:gcol + 4 * q + 4], [[1, 4], [0, 128]])
                S.op("dve", lambda e: e.tensor_tensor(
                    out=dst[:, 4 * q:4 * q + 4, dst_col:dst_col + 128],
                    in0=pb[:].rearrange("p (a b) -> p a b", a=4), in1=g_b, op=ALU.mult),
                    reads=[pk, "vecs"], writes=[dst_key])

        HKS = [("hTt0",), ("hTt1",)]
        tiles = [(128 + 128 * t, hT, 128 * t, ("hT", t // 4)) for t in range(8)]
        tiles += [(1152 + 128 * t, hTts[t // 4], 128 * (t % 4), HKS[t // 4]) for t in range(8)]
        tiles += [(0, hTts[0], 0, HKS[0]), (2176, hTts[0], 128, HKS[0])]

        def after_tile(t):
            if t == 7:
                for blk in range(2):
                    usproj(hT[:, :, 512 * blk:512 * blk + 512], ("hT", blk), 512, 128 + 512 * blk)
            if t == 11:
                S.op("dve", lambda e: e.tensor_copy(out=hT[:, :, 1025:1026], in_=hTts[0][:, :, 0:1]),
                     reads=[HKS[0]], writes=[("hT", 2)])
                usproj(hTts[0], HKS[0], 512, 1152)
            if t == 15:
                usproj(hTts[1], HKS[1], 512, 1664)
            if t == 17:
                hTt = hTts[0]
                S.op("dve", lambda e: e.tensor_copy(out=hT[:, :, 1024:1025], in_=hTt[:, :, 127:128]),
                     reads=[HKS[0]], writes=[("hT", 2)])
                for m in range(8):
                    pb, pk = bank()
                    for k in range(16):
                        S.op("pe", lambda e: e.matmul(
                            pb[:, 0:256], lhsT=wus[:, k, m * 128:(m + 1) * 128], rhs=hTt[:, k, 0:256],
                            start=(k == 0), stop=(k == 15)),
                            reads=[("wus", k // 4), HKS[0]], writes=[pk])
                    S.op("act", lambda e: e.copy(
                        out=mk(u_fm[:, m, :], [[NCH, 8], [1, 16]], off=0),
                        in_=mk(pb[:, 0:128], [[1, 8], [8, 16]])),
                         reads=[pk], writes=[("u_fm", m)])
                    S.op("act", lambda e: e.copy(
                        out=mk(u_fm[:, m, :], [[NCH, 8], [1, 16]], off=272),
                        in_=mk(pb[:, 0:256], [[1, 8], [8, 16]], off=128)),
                         reads=[pk], writes=[("u_fm", m)])

        norm_stats(0, tiles[0][0])
        norm_stats(1, tiles[1][0])
        for t in range(len(tiles)):
            if t + 2 < len(tiles):
                norm_stats(t + 2, tiles[t + 2][0])
            row0, dst, col, key = tiles[t]
            norm_trans(t, dst, col, V_GPRE, key)
            after_tile(t)
        checkpoint("A0")
        dump("u", u_fm[:].rearrange("p a b -> p (a b)"), [128, 8 * NSEQ], ("u_fm", 7), BF16)

        checkpoint("A")
        A.release(m2)
        pA = A.alloc("pA", [128, 4288], F32)
        S.dma("sp", pA[:], pA_d, "pA", writes=["pA"])
        LRE, LIM, LDT, BR, BI, CR, CI = 0, 64, 128, 192, 1216, 2240, 3264
        tbK = A.alloc("tbK", [128, 2, 17 * 64], F32)
        sm = A.alloc("sm", [128, 24, 64], F32)
        wtab = A.alloc("wtab", [128, 2, 8 * 64], F32)
        mt = A.mark()
        tbT = A.alloc("tbT", [128, 6, 17 * 64], F32)
        T_ARG, T_E, T_S, T_C, T_T1, T_T2, T_ARE, T_AIM = range(8)

        def tsl(i):
            return tbT[:, i, :] if i < 6 else tbK[:, i - 6, :]

        def TK(i):
            return ("tbT", i) if i < 6 else ("tbK", i - 6)

        def tbv(i, m0=0, m1=17):
            return tsl(i)[:, m0 * 64:m1 * 64]

        (s_lr, s_li, s_dt, s_ldt, s_th, s_zr, s_zi, s_den, s_t1, s_t2, s_t3, s_rho, s_phi,
         s_c8, s_s8, s_ns8) = range(16)
        def tt(eng, out, a, b, op, reads, writes):
            S.op(eng, lambda e: e.tensor_tensor(out=out, in0=a, in1=b, op=op), reads=reads, writes=writes)

        I32 = mybir.dt.int32
        halfpi = small[:, 33:34]
        S.op("dve", lambda e: e.memset(halfpi, math.pi / 2), writes=[("small", 98)])

        def range_reduce(r, x, qi, qf, xk, rk, qik, qfk):
            S.op("dve", lambda e: e.tensor_scalar(out=qi, in0=x, scalar1=1.0 / TWO_PI, scalar2=None, op0=ALU.mult),
                 reads=[xk, qik], writes=[qik])
            S.op("dve", lambda e: e.scalar_tensor_tensor(out=r, in0=qi, scalar=-TWO_PI, in1=x, op0=ALU.mult, op1=ALU.add),
                 reads=[qik, xk, rk], writes=[rk])

        SIN_S = 1.0 - 1e-5

        def sincos(sn, co, r, tmp, rk, snk, cok, tmpk):
            S.op("act", lambda e: e.activation(out=sn, in_=r, func=AF.Sin, scale=SIN_S), reads=[rk, snk], writes=[snk])
            S.op("act", lambda e: e.activation(out=tmp, in_=r, func=AF.Abs), reads=[rk, tmpk], writes=[tmpk])
            S.op("act", lambda e: e.activation(out=co, in_=tmp, func=AF.Sin, bias=halfpi, scale=-SIN_S),
                 reads=[tmpk, cok, ("small", 98)], writes=[cok])

        K = "sm"
        S.op("dve", lambda e: e.tensor_scalar(out=sm[:, s_lr, :], in0=pA[:, LRE:LRE + 64], scalar1=-1e-4,
                                              scalar2=None, op0=ALU.min), reads=["pA"], writes=[K])
        S.op("act", lambda e: e.activation(out=sm[:, s_dt, :], in_=pA[:, LDT:LDT + 64], func=AF.Exp),
             reads=["pA"], writes=[K])
        tt("dve", sm[:, s_ldt, :], sm[:, s_lr, :], sm[:, s_dt, :], ALU.mult, [K], [K])
        tt("dve", sm[:, s_th, :], pA[:, LIM:LIM + 64], sm[:, s_dt, :], ALU.mult, [K, "pA"], [K])
        mv_b = mk(mvals, [[1, 17], [0, 64]])
        ldt_b = mk(sm[:, s_ldt, :], [[0, 17], [1, 64]])
        th_b = mk(sm[:, s_th, :], [[0, 17], [1, 64]])
        v3 = lambda i: tsl(i).rearrange("p (m n) -> p m n", m=17)
        tt("dve", v3(T_ARG), mv_b, ldt_b, ALU.mult, [K, "cF"], [TK(T_ARG)])
        S.op("act", lambda e: e.activation(out=tbv(T_E), in_=tbv(T_ARG), func=AF.Exp),
             reads=[TK(T_ARG)], writes=[TK(T_E)])
        tt("dve", v3(T_ARG), mv_b, th_b, ALU.mult, [K, "cF", TK(T_ARG)], [TK(T_ARG)])
        qi_s = A.alloc("qi_s", [128, 17 * 64], I32)
        range_reduce(tbv(T_T1), tbv(T_ARG), qi_s[:], tbv(T_T2), TK(T_ARG), TK(T_T1), "qi_s", TK(T_T2))
        sincos(tbv(T_S), tbv(T_C), tbv(T_T1), tbv(T_T2), TK(T_T1), TK(T_S), TK(T_C), TK(T_T2))
        tt("dve", tbv(T_ARE), tbv(T_E), tbv(T_C), ALU.mult, [TK(T_E), TK(T_C)], [TK(T_ARE)])
        tt("dve", tbv(T_AIM), tbv(T_E), tbv(T_S), ALU.mult, [TK(T_E), TK(T_S)], [TK(T_AIM)])
        ARE, AIM = TK(T_ARE), TK(T_AIM)
        are1 = tbv(T_ARE, 0, 1)
        aim1 = tbv(T_AIM, 0, 1)
        tt("dve", sm[:, s_t1, :], sm[:, s_lr, :], sm[:, s_lr, :], ALU.mult, [K], [K])
        tt("dve", sm[:, s_t2, :], pA[:, LIM:LIM + 64], pA[:, LIM:LIM + 64], ALU.mult, ["pA"], [K])
        tt("dve", sm[:, s_den, :], sm[:, s_t1, :], sm[:, s_t2, :], ALU.add, [K], [K])
        S.op("dve", lambda e: e.reciprocal(out=sm[:, s_den, :], in_=sm[:, s_den, :]), reads=[K], writes=[K])
        S.op("dve", lambda e: e.tensor_scalar(out=sm[:, s_t3, :], in0=are1, scalar1=-1.0, scalar2=None,
                                              op0=ALU.add), reads=[ARE], writes=[K])
        tt("dve", sm[:, s_t1, :], sm[:, s_t3, :], sm[:, s_lr, :], ALU.mult, [K], [K])
        tt("dve", sm[:, s_t2, :], aim1, pA[:, LIM:LIM + 64], ALU.mult, [AIM, "pA"], [K])
        tt("dve", sm[:, s_zr, :], sm[:, s_t1, :], sm[:, s_t2, :], ALU.add, [K], [K])
        tt("dve", sm[:, s_zr, :], sm[:, s_zr, :], sm[:, s_den, :], ALU.mult, [K], [K])
        tt("dve", sm[:, s_t1, :], aim1, sm[:, s_lr, :], ALU.mult, [AIM, K], [K])
        tt("dve", sm[:, s_t2, :], sm[:, s_t3, :], pA[:, LIM:LIM + 64], ALU.mult, [K, "pA"], [K])
        tt("dve", sm[:, s_zi, :], sm[:, s_t1, :], sm[:, s_t2, :], ALU.subtract, [K], [K])
        tt("dve", sm[:, s_zi, :], sm[:, s_zi, :], sm[:, s_den, :], ALU.mult, [K], [K])
        zr_b = mk(sm[:, s_zr, :], [[0, 8], [1, 64]])
        zi_b = mk(sm[:, s_zi, :], [[0, 8], [1, 64]])
        an_re = tsl(T_ARE)[:, 8 * 64:16 * 64].rearrange("p (m n) -> p m n", m=8)
        an_im = tsl(T_AIM)[:, 8 * 64:16 * 64].rearrange("p (m n) -> p m n", m=8)
        t1v = tsl(T_T1)[:, 0:512].rearrange("p (m n) -> p m n", m=8)
        t2v = tsl(T_T2)[:, 0:512].rearrange("p (m n) -> p m n", m=8)
        wr_v = wtab[:, 0, :].rearrange("p (m n) -> p m n", m=8)
        wi_v = wtab[:, 1, :].rearrange("p (m n) -> p m n", m=8)
        tt("dve", t1v, an_re, zr_b, ALU.mult, [ARE, K, TK(T_T1)], [TK(T_T1)])
        tt("dve", t2v, an_im, zi_b, ALU.mult, [AIM, K, TK(T_T2)], [TK(T_T2)])
        tt("dve", wr_v, t1v, t2v, ALU.subtract, [TK(T_T1), TK(T_T2)], ["wtab"])
        tt("dve", t1v, an_re, zi_b, ALU.mult, [ARE, K, TK(T_T1)], [TK(T_T1)])
        tt("dve", t2v, an_im, zr_b, ALU.mult, [AIM, K, TK(T_T2)], [TK(T_T2)])
        tt("dve", wi_v, t1v, t2v, ALU.add, [TK(T_T1), TK(T_T2), "wtab"], ["wtab"])
        S.op("act", lambda e: e.copy(out=sm[:, s_rho, :], in_=tbv(T_E, 16, 17)), reads=[TK(T_E)], writes=[K])
        S.op("act", lambda e: e.copy(out=sm[:, s_c8, :], in_=tbv(T_ARE, 16, 17)), reads=[ARE], writes=[K])
        S.op("act", lambda e: e.copy(out=sm[:, s_s8, :], in_=tbv(T_AIM, 16, 17)), reads=[AIM], writes=[K])
        S.op("act", lambda e: e.mul(out=sm[:, s_ns8, :], in_=tbv(T_AIM, 16, 17), mul=-1.0), reads=[AIM], writes=[K])
        S.op("dve", lambda e: e.tensor_scalar(out=sm[:, s_t1, :], in0=sm[:, s_th, :], scalar1=8.0, scalar2=2 * TWO_PI,
                                              op0=ALU.mult, op1=ALU.add), reads=[K], writes=[K])
        range_reduce(sm[:, s_phi, :], sm[:, s_t1, :], qi_s[:, 0:64], sm[:, s_t2, :], K, K, "qi_s", K)

        A.release(mt)
        dump("are", tsl(T_ARE), [128, 1088], TK(T_ARE))
        dump("aim", tsl(T_AIM), [128, 1088], TK(T_AIM))
        dump("sm", sm[:].rearrange("p a b -> p (a b)"), [128, 24 * 64], K)
        dump("wtab", wtab[:].rearrange("p a b -> p (a b)"), [128, 1024], "wtab")
        checkpoint("T")
        Wre = [A.alloc("Wre%d" % i, [128, 8, 128], BF16) for i in range(2)]
        Wim = [A.alloc("Wim%d" % i, [128, 8, 128], BF16) for i in range(2)]
        M1re = [A.alloc("M1re%d" % i, [128, 8, 128], BF16) for i in range(2)]
        M1im = [A.alloc("M1im%d" % i, [128, 8, 128], BF16) for i in range(2)]
        Mst = [A.alloc("Mst%d" % i, [128, 16, 256], BF16) for i in range(2)]
        _dm = A.alloc("Dm0", [128, 8, 3, 64], BF16)
        Dm = [_dm, _dm]
        tmpA = A.alloc("tmpA", [128, 512], F32)
        tmpB = A.alloc("tmpB", [128, 512], F32)
        U_bs = [A.alloc("U_b%d" % i, [128, 8, NCH], BF16) for i in range(2)]
        NB = 4 * 272
        SR, SI, CO, SN, T1, T2 = [A.alloc(nm, [128, NB], F32) for nm in ("SR", "SI", "CO", "SN", "T1", "T2")]
        Gre = A.alloc("Gre", [128, 8, 128], BF16)
        Gim = A.alloc("Gim", [128, 8, 128], BF16)
        Yb = A.alloc("Yb", [128, 8, 128], BF16)
        ytmp = T1[:, 0:NTOK]
        ytmp2 = T2[:, 0:NTOK]
        NPOS = (144, 272)

        def slotAP(tbl_i, base_slot, d, n0, rev):
            v = tsl(tbl_i)
            if not rev:
                return mk(v, [[1, 4], [64, 8], [0, 16]], off=base_slot * 64 + n0)
            return mk(v, [[1, 4], [-64, 8], [0, 16]], off=(base_slot + 7) * 64 + n0)

        def stage_P1(b, bf, dsel):
            KW = lambda nm, d: ("%s%d" % (nm, bf), d)
            for d in (dsel,):
                n0 = d * 32 + b * 4
                rev = (d == 1)
                cr = mk(pA[:, CR:CR + 1024], [[16, 4], [0, 8], [1, 16]], off=n0 * 16)
                ci = mk(pA[:, CI:CI + 1024], [[16, 4], [0, 8], [1, 16]], off=n0 * 16)
                br = mk(pA[:, BR:BR + 1024], [[16, 4], [0, 8], [1, 16]], off=n0 * 16)
                bi = mk(pA[:, BI:BI + 1024], [[16, 4], [0, 8], [1, 16]], off=n0 * 16)
                a_re = slotAP(T_ARE, 0, d, n0, rev)
                a_im = slotAP(T_AIM, 0, d, n0, rev)
                wv = wtab[:, 0, :]
                if not rev:
                    w_r = mk(wv, [[1, 4], [64, 8], [0, 16]], off=n0)
                    w_i = mk(wv, [[1, 4], [64, 8], [0, 16]], off=512 + n0)
                else:
                    w_r = mk(wv, [[1, 4], [-64, 8], [0, 16]], off=7 * 64 + n0)
                    w_i = mk(wv, [[1, 4], [-64, 8], [0, 16]], off=512 + 7 * 64 + n0)
                tA = tmpA[:, 0:512].rearrange("p (a b c) -> p a b c", a=4, b=8)
                tB = tmpB[:, 0:512].rearrange("p (a b c) -> p a b c", a=4, b=8)
                r4 = lambda t: t[bf][:, 4 * d:4 * d + 4, :].rearrange("p a (b c) -> p a b c", b=8)
                RD = ["pA", ARE, AIM, "wtab"]
                E_ = "dve"
                tt(E_, tA, cr, a_re, ALU.mult, RD + ["tmpA"], ["tmpA"])
                tt(E_, tB, ci, a_im, ALU.mult, RD + ["tmpB"], ["tmpB"])
                tt(E_, r4(M1re), tA, tB, ALU.subtract, ["tmpA", "tmpB"], [KW("M1re", d)])
                tt(E_, tA, cr, a_im, ALU.mult, RD + ["tmpA"], ["tmpA"])
                tt(E_, tB, ci, a_re, ALU.mult, RD + ["tmpB"], ["tmpB"])
                tt(E_, tA, tA, tB, ALU.add, ["tmpA", "tmpB"], ["tmpA"])
                S.op(E_, lambda e: e.tensor_scalar(out=r4(M1im), in0=tA, scalar1=-1.0, scalar2=None, op0=ALU.mult),
                     reads=["tmpA"], writes=[KW("M1im", d)])
                tt(E_, tA, br, w_r, ALU.mult, RD + ["tmpA"], ["tmpA"])
                tt(E_, tB, bi, w_i, ALU.mult, RD + ["tmpB"], ["tmpB"])
                tt(E_, r4(Wre), tA, tB, ALU.subtract, ["tmpA", "tmpB"], [KW("Wre", d)])
                tt(E_, tA, bi, w_r, ALU.mult, RD + ["tmpA"], ["tmpA"])
                tt(E_, tB, br, w_i, ALU.mult, RD + ["tmpB"], ["tmpB"])
                tt(E_, r4(Wim), tA, tB, ALU.add, ["tmpA", "tmpB"], [KW("Wim", d)])

        def stage_P2a(b, bf):
            U_b = U_bs[bf]
            UK = "U_b%d" % bf
            for nl in range(8):
                d, gq = nl // 4, nl % 4
                n = d * 32 + b * 4 + gq
                for (j3, col) in ((0, s_c8), (1, s_s8), (2, s_ns8)):
                    S.op("act", lambda e: e.activation(
                        out=Dm[bf][:, nl, j3, :], in_=I2, func=AF.Copy, scale=sm[:, col, n:n + 1]),
                        reads=[K, "cF"], writes=[("Dm0", nl)])
            ub = u_fm[:, b, :]
            for g8 in range(8):
                pb, pk = bank()
                for j in range(8):
                    rhs = ub[:, j * NCH:(j + 1) * NCH]
                    S.op("pe", lambda e: e.matmul(
                        pb[:, 0:NCH], lhsT=strips[:, g8, 16 * (7 - j):16 * (7 - j) + 128], rhs=rhs,
                        start=(j == 0), stop=(j == 7)),
                        reads=[("u_fm", b), "strips"], writes=[pk])
                S.op("act", lambda e: e.copy(out=U_b[:, g8, :], in_=pb[:, 0:NCH]),
                     reads=[pk], writes=[(UK, g8)])

        def stage_P2b(b, bf, dsel):
            KW = lambda nm, d: ("%s%d" % (nm, bf), d)
            for nl in range(4 * dsel, 4 * dsel + 4):
                d = nl // 4
                for hh in range(2):
                    r0, r1 = 64 * hh, 64 * hh + 64
                    pb, pk = bank()
                    wre, wim = Wre[bf][r0:r1, nl, :], Wim[bf][r0:r1, nl, :]
                    mm = [
                        (pb[:, 0:128], wre, M1re[bf][r0:r1, nl, :], True, False),
                        (pb[:, 0:128], wim, M1im[bf][r0:r1, nl, :], False, True),
                        (pb[:, 128:192], wre, Dm[bf][r0:r1, nl, 0, :], True, False),
                        (pb[:, 128:192], wim, Dm[bf][r0:r1, nl, 2, :], False, True),
                        (pb[:, 192:256], wre, Dm[bf][r0:r1, nl, 1, :], True, False),
                        (pb[:, 192:256], wim, Dm[bf][r0:r1, nl, 0, :], False, True),
                    ]
                    for (o, l, r, st, sp) in mm:
                        S.op("pe", lambda e: e.matmul(o, lhsT=l, rhs=r, start=st, stop=sp),
                             reads=[KW("Wre", d), KW("Wim", d), KW("M1re", d), KW("M1im", d), ("Dm0", nl)],
                             writes=[pk])
                    msk = maskA if d == 0 else maskB
                    slot = 2 * nl + hh
                    S.op("act", lambda e: e.copy(out=Mst[bf][:, slot, :], in_=pb[:, 0:256]),
                         reads=[pk], writes=[("Mst%d" % bf, slot)])
                    S.op("pool", lambda e: e.tensor_tensor(
                        out=Mst[bf][:, slot, 0:128], in0=Mst[bf][:, slot, 0:128], in1=msk, op=ALU.mult),
                        reads=[("Mst%d" % bf, slot), "cF"], writes=[("Mst%d" % bf, slot)])

        def stage_D(b, bf, d):
            U_b = U_bs[bf]
            UK = "U_b%d" % bf
            if True:
                NP = NPOS[d]
                W4 = 4 * NP
                n0 = d * 32 + b * 4
                v2 = lambda t: t[:, 0:W4].rearrange("p (a k) -> p a k", a=4)
                fl = lambda t: t[:, 0:W4]
                phib = mk(sm[:, s_phi, :], [[1, 4], [0, NP]], off=n0)
                kb = mk(kpos, [[0, 4], [1, NP]])
                tt("dve", v2(T1), phib, kb, ALU.mult, [K, "cF", "T1"], ["T1"])
                range_reduce(fl(T2), fl(T1), fl(SR).bitcast(I32), fl(SI), "T1", "T2", "SR", "SI")
                sincos(fl(SN), fl(CO), fl(T2), fl(SI), "T2", "SN", "CO", "SI")
                for gq in range(4):
                    pr, pkr = bank()
                    pi_, pki = bank()
                    nl = 4 * d + gq
                    for hh in range(2):
                        g8 = 4 * hh + gq
                        slot = 2 * nl + hh
                        uv = U_b[:, g8, :]
                        rhs = uv[:, 0:NP] if d == 0 else mk(uv, [[-1, NP]], off=NCH - 1)
                        S.op("pe", lambda e: e.matmul(
                            pr[64 * hh:64 * hh + 64, 0:NP], lhsT=Mst[bf][:, slot, 128:192], rhs=rhs, start=True, stop=True),
                            reads=[("Mst%d" % bf, slot), (UK, g8)], writes=[pkr])
                        S.op("pe", lambda e: e.matmul(
                            pi_[64 * hh:64 * hh + 64, 0:NP], lhsT=Mst[bf][:, slot, 192:256], rhs=rhs, start=True, stop=True),
                            reads=[("Mst%d" % bf, slot), (UK, g8)], writes=[pki])
                    S.op("act", lambda e: e.copy(out=SR[:, gq * NP:(gq + 1) * NP], in_=pr[:, 0:NP]),
                         reads=[pkr, "SR"], writes=["SR"])
                    S.op("act", lambda e: e.copy(out=SI[:, gq * NP:(gq + 1) * NP], in_=pi_[:, 0:NP]),
                         reads=[pki, "SI"], writes=["SI"])
                tt("dve", fl(T1), fl(SR), fl(SN), ALU.mult, ["SR", "SN", "T1"], ["T1"])
                tt("dve", fl(SR), fl(SR), fl(CO), ALU.mult, ["SR", "CO"], ["SR"])
                tt("dve", fl(T2), fl(SI), fl(SN), ALU.mult, ["SI", "SN", "T2"], ["T2"])
                tt("dve", fl(SI), fl(SI), fl(CO), ALU.mult, ["SI", "CO"], ["SI"])
                tt("dve", fl(SR), fl(SR), fl(T2), ALU.add, ["SR", "T2"], ["SR"])
                tt("dve", fl(SI), fl(SI), fl(T1), ALU.subtract, ["SI", "T1"], ["SI"])
                rhob = mk(sm[:, s_rho, :], [[1, 4], [0, NP]], off=n0)
                S.op("dve", lambda e: e.tensor_copy(out=v2(T1), in_=rhob), reads=[K, "T1"], writes=["T1"])
                S.op("dve", lambda e: e.memset(mk(T1[:], [[NP, 4], [1, 1]]), 0.0), reads=["T1"], writes=["T1"])
                S.op("dve", lambda e: e.tensor_tensor_scan(out=fl(T2), data0=fl(T1), data1=fl(SR), initial=0.0,
                                                           op0=ALU.mult, op1=ALU.add),
                     reads=["T1", "SR", "T2"], writes=["T2"])
                S.op("dve", lambda e: e.tensor_tensor_scan(out=fl(SR), data0=fl(T1), data1=fl(SI), initial=0.0,
                                                           op0=ALU.mult, op1=ALU.add),
                     reads=["T1", "SI", "SR"], writes=["SR"])
                k0 = 16 if d == 0 else 144
                sel = lambda t: mk(t[:], [[NP, 4], [1, 128]], off=k0 - 1)
                a3 = lambda t: t[:, 0:512].rearrange("p (a k) -> p a k", a=4)
                tt("dve", a3(SI), sel(T2), sel(CO), ALU.mult, ["T2", "CO", "SI"], ["SI"])
                tt("dve", a3(T1), sel(SR), sel(SN), ALU.mult, ["SR", "SN", "T1"], ["T1"])
                tt("dve", Gre[:, 4 * d:4 * d + 4, :], a3(SI), a3(T1), ALU.subtract, ["SI", "T1"], [("Gre", d)])
                tt("dve", a3(SI), sel(T2), sel(SN), ALU.mult, ["T2", "SN", "SI"], ["SI"])
                tt("dve", a3(T1), sel(SR), sel(CO), ALU.mult, ["SR", "CO", "T1"], ["T1"])
                tt("dve", Gim[:, 4 * d:4 * d + 4, :], a3(SI), a3(T1), ALU.add, ["SI", "T1"], [("Gim", d)])

        def stage_Y(b, bf):
            U_b = U_bs[bf]
            UK = "U_b%d" % bf
            for g8 in range(8):
                hh, gq = g8 // 4, g8 % 4
                r0, r1 = 64 * hh, 64 * hh + 64
                pb, pk = bank()
                first = True
                for d in range(2):
                    nl = 4 * d + gq
                    slot = 2 * nl + hh
                    u_own = U_b[:, g8, 16:144]
                    if d == 0:
                        gre, gim = Gre[r0:r1, nl, :], Gim[r0:r1, nl, :]
                    else:
                        gre = mk(Gre[r0:r1, nl, :], [[-1, 128]], off=127)
                        gim = mk(Gim[r0:r1, nl, :], [[-1, 128]], off=127)
                    seq = [(Mst[bf][:, slot, 0:128], u_own), (M1re[bf][r0:r1, nl, :], gre), (M1im[bf][r0:r1, nl, :], gim)]
                    for qi, (l, r) in enumerate(seq):
                        last = (d == 1 and qi == 2)
                        S.op("pe", lambda e: e.matmul(pb[:, 0:128], lhsT=l, rhs=r, start=first, stop=last),
                             reads=[("Mst%d" % bf, slot), (UK, g8), ("M1re%d" % bf, d), ("M1im%d" % bf, d),
                                    ("Gre", d), ("Gim", d)], writes=[pk])
                        first = False
                S.op("act", lambda e: e.copy(out=Yb[:, g8, :], in_=pb[:, 0:128]),
                     reads=[pk], writes=[("Yb", g8)])
            pbs = [(ps[6], ("ps", 6)), (ps[7], ("ps", 7))]
            for i in range(8):
                for g8 in range(8):
                    for half in range(2):
                        pb, pk = pbs[half]
                        o = mk(pb[:, 0:512], [[8, 64]], off=i)
                        S.op("pe", lambda e: e.matmul(
                            o, lhsT=strips[:, i, 16 * (7 - g8):16 * (7 - g8) + 128],
                            rhs=Yb[:, g8, 64 * half:64 * half + 64],
                            start=(i == 0 and g8 == 0), stop=(i == 7 and g8 == 7), skip_group_check=True),
                            reads=[("Yb", g8), "strips"], writes=[pk])
            return pbs

        def stage_E(b, pbs):
            dsk = vecs[:, V_DSK + b:V_DSK + b + 1]
            for half in range(2):
                pb, pk = pbs[half]
                c0 = 512 * half
                S.op("dve", lambda e: e.scalar_tensor_tensor(
                    out=ytmp[:, c0:c0 + 512].rearrange("p (c j) -> p c j", j=8),
                    in0=mk(u_fm[:, b, :], [[1, 64], [NCH, 8]], off=16 + 64 * half), scalar=dsk,
                    in1=pb[:, 0:512].rearrange("p (c j) -> p c j", j=8), op0=ALU.mult, op1=ALU.add),
                    reads=[pk, ("u_fm", b), "vecs", "T1"], writes=["T1"])
            if b == 0:
                dump("y0", ytmp, [128, NTOK], "T1")
                checkpoint("S0")
            S.op("act", lambda e: e.copy(out=gT[:, b, :], in_=ytmp), reads=["T1"], writes=[("gT", b)])

        stage_P1(0, 0, 0)
        stage_P1(0, 0, 1)
        stage_P2a(0, 0)
        stage_P2b(0, 0, 0)
        stage_P2b(0, 0, 1)
        for b in range(8):
            nb_ = (b + 1) % 2
            more = b < 7
            stage_D(b, b % 2, 0)
            if more:
                stage_P2a(b + 1, nb_)
                stage_P1(b + 1, nb_, 0)
            stage_D(b, b % 2, 1)
            if more:
                stage_P2b(b + 1, nb_, 0)
                stage_P1(b + 1, nb_, 1)
            pbs = stage_Y(b, b % 2)
            stage_E(b, pbs)
            if more:
                stage_P2b(b + 1, nb_, 1)
        print("S5 arena peak", A.peak - A.LO, "of", A.HI - A.LO)
        ps_pool[0] = list(range(8))

        checkpoint("S")
        dump("g", gT[:].rearrange("p a b -> p (a b)"), [128, 8 * NTOK], ("gT", 7), BF16)

        A.release(m1)
        g2 = gT
        G2 = [("gT", i) for i in range(8)]
        cv = A.alloc("cv", [128, 8, NTOK], BF16)
        mgd = A.alloc("mg", [128, 16, NTOK], BF16)
        m_mid = A.mark()

        wcv = [A.alloc("wcv%d" % i, [128, 3, 16, 128], BF16) for i in range(1)]
        _lo = A.addr["wcv0"]
        if _lo >= A.addr["pA"] and _lo + 12288 <= A.addr["tbK"] + 2 * 17 * 64 * 4:
            S.inherit["wcv0"] = S.all_events({"pA", "tbK"})
        else:
            print("note: wcv0 not on pA/tbK bytes, no early prefetch", _lo, A.addr["pA"], A.addr["tbK"])
        wcv += [A.alloc("wcv%d" % i, [128, 3, 16, 128], BF16) for i in range(1, 3)]
        TOPSLOT = 227776 - 16384
        wgl = A.fixed("wgl", [128, 8, 1024], BF16, TOPSLOT)
        xcs = A.alloc("xcs", [128, 1026], F32)
        zx = A.alloc("zx", [128, 1026], F32)
        vv = A.alloc("vv", [128, NTOK], F32)
        HT = [("hT", 0), ("hT", 1), ("hT", 2)]
        gtmp = [(A.alloc("gtmp%d" % i, [128, NTOK], F32)[:], "gtmp%d" % i) for i in range(4)]

        def gelu_b(b):
            tq, tk = gtmp[b % 4]
            S.op("act", lambda e: e.activation(out=tq, in_=gT[:, b, :], func=AF.Square), reads=[("gT", b), tk], writes=[tk])
            S.op("dve", lambda e: e.tensor_scalar(out=tq, in0=tq, scalar1=0.044715, scalar2=1.0,
                                                  op0=ALU.mult, op1=ALU.add), reads=[tk], writes=[tk])
            tt("dve", tq, tq, gT[:, b, :], ALU.mult, [tk, ("gT", b)], [tk])
            S.op("act", lambda e: e.activation(out=tq, in_=tq, func=AF.Sigmoid, scale=1.5957691216057308),
                 reads=[tk], writes=[tk])
            tt("dve", gT[:, b, :], tq, gT[:, b, :], ALU.mult, [tk, ("gT", b)], [("gT", b)])

        for j in range(8):
            wb = wcv[j % 3]
            wk = ("wcv%d" % (j % 3),)
            for s in range(3):
                c0 = 1024 + 1024 * s + 128 * j
                S.dma("pool", wb[:, s, :, :], w_in_v[:, :, c0:c0 + 128], "wcv%d" % (j % 3), writes=[wk])
            if j == 7:
                for kq in range(2):
                    S.dma("pool", wgl[:, 4 * kq:4 * kq + 4, :], w_glu_v[:, 4 * kq:4 * kq + 4, :], "wgl",
                          writes=[("wgl", kq)])


            banks = {}
            for s in range(3):
                for blk in range(2):
                    pb, pk = bank()
                    banks[(s, blk)] = (pb, pk)
                    for k in range(16):
                        S.op("pe", lambda e, pb=pb, s=s, k=k, blk=blk, wb=wb: e.matmul(
                            pb[:, 0:512], lhsT=wb[:, s, k, :], rhs=hT[:, k, 512 * blk:512 * blk + 512],
                            start=(k == 0), stop=(k == 15)), reads=[wk] + HT, writes=[pk])
            pbh, pkh = bank()
            for si, s in enumerate((0, 2)):
                for k in range(16):
                    S.op("pe", lambda e, si=si, s=s, k=k, wb=wb: e.matmul(
                        pbh[:, 2 * si:2 * si + 2], lhsT=wb[:, s, k, :], rhs=hT[:, k, 1024:1026],
                        start=(k == 0), stop=(k == 15)), reads=[wk] + HT, writes=[pkh])
            for blk in range(2):
                pb, pk = banks[(0, blk)]
                S.op("act", lambda e, pb=pb, blk=blk: e.copy(out=xcs[:, 1 + 512 * blk:513 + 512 * blk], in_=pb[:, 0:512]),
                     reads=[pk], writes=[("xcs", blk)])
            S.op("act", lambda e: e.copy(out=xcs[:, 0:1], in_=pbh[:, 0:1]), reads=[pkh], writes=[("xcs", 2)])
            S.op("act", lambda e: e.copy(out=xcs[:, 1025:1026], in_=pbh[:, 1:2]), reads=[pkh], writes=[("xcs", 3)])
            for blk in range(2):
                pb, pk = banks[(2, blk)]
                S.op("dve", lambda e, pb=pb, blk=blk: e.tensor_tensor(
                    out=zx[:, 1 + 512 * blk:513 + 512 * blk], in0=pb[:, 0:512], in1=xcs[:, 1 + 512 * blk:513 + 512 * blk],
                    op=ALU.mult), reads=[pk, ("xcs", blk)], writes=[("zx", blk)])
            S.op("dve", lambda e: e.tensor_tensor(out=zx[:, 0:1], in0=pbh[:, 2:3], in1=xcs[:, 0:1], op=ALU.mult),
                 reads=[pkh, ("xcs", 2)], writes=[("zx", 2)])
            S.op("dve", lambda e: e.tensor_tensor(out=zx[:, 1025:1026], in0=pbh[:, 3:4], in1=xcs[:, 1025:1026], op=ALU.mult),
                 reads=[pkh, ("xcs", 3)], writes=[("zx", 3)])
            ZK = [("zx", i) for i in range(4)]
            w0 = vecs[:, V_CW + j:V_CW + j + 1]
            w1 = vecs[:, V_CW + 8 + j:V_CW + 8 + j + 1]
            w2 = vecs[:, V_CW + 16 + j:V_CW + 16 + j + 1]
            cb = vecs[:, V_CB + j:V_CB + j + 1]
            S.op("act", lambda e: e.activation(out=vv[:], in_=zx[:, 1:1025], func=AF.Identity, bias=cb, scale=w1),
                 reads=ZK + ["vecs", "vv"], writes=["vv"])
            S.op("dve", lambda e: e.scalar_tensor_tensor(out=vv[:], in0=zx[:, 0:1024], scalar=w0, in1=vv[:],
                                                         op0=ALU.mult, op1=ALU.add), reads=ZK + ["vv", "vecs"], writes=["vv"])
            S.op("dve", lambda e: e.scalar_tensor_tensor(out=vv[:], in0=zx[:, 2:1026], scalar=w2, in1=vv[:],
                                                         op0=ALU.mult, op1=ALU.add), reads=ZK + ["vv", "vecs"], writes=["vv"])
            for blk in range(2):
                pb, pk = banks[(1, blk)]
                S.op("dve", lambda e, pb=pb, blk=blk, j=j: e.tensor_tensor(
                    out=cv[:, j, 512 * blk:512 * blk + 512], in0=pb[:, 0:512], in1=vv[:, 512 * blk:512 * blk + 512],
                    op=ALU.mult), reads=[pk, "vv"], writes=[("cv", j)])
            gelu_b(j)
            if j == 5:
                pass
        checkpoint("C0")
        dump("cv", cv[:].rearrange("p a b -> p (a b)"), [128, 8 * NTOK], ("cv", 7), BF16)
        A.release(m_mid)

        g3 = A.alloc("g3", [128, 8, NTOK], BF16)
        sg = A.alloc("sg", [128, 2, 512], F32)
        for m in range(8):
            for blk in range(2):
                pb, pk = bank()
                for k in range(8):
                    S.op("pe", lambda e, pb=pb, k=k, m=m, blk=blk: e.matmul(
                        pb[:, 0:512], lhsT=wgl[:, k, 128 * m:128 * m + 128], rhs=g2[:, k, 512 * blk:512 * blk + 512],
                        start=(k == 0), stop=(k == 7)), reads=[("wgl", 0), ("wgl", 1)] + G2, writes=[pk])
                S.op("act", lambda e, pb=pb, blk=blk, m=m: e.activation(
                    out=sg[:, blk, :], in_=pb[:, 0:512], func=AF.Sigmoid, bias=vecs[:, V_BGLU + m:V_BGLU + m + 1]),
                    reads=[pk, "vecs", ("sg", blk)], writes=[("sg", blk)])
                S.op("dve", lambda e, blk=blk, m=m: e.tensor_tensor(
                    out=g3[:, m, 512 * blk:512 * blk + 512], in0=sg[:, blk, :], in1=g2[:, m, 512 * blk:512 * blk + 512],
                    op=ALU.mult), reads=[("sg", blk)] + G2, writes=[("g3", m)])
        G3 = [("g3", m) for m in range(8)]
        CV = [("cv", m) for m in range(8)]

        wmg = [A.alloc("wmg%d" % i, [128, 48, 256], BF16) for i in range(2)]
        et = A.alloc("et", [128, 2, 4, 512], F32)
        A.retire_fixed(["wgl"], TOPSLOT, TOPSLOT + 16384)
        wo0 = A.fixed("wo0", [128, 16, 512], BF16, TOPSLOT)
        for m in range(16):
            mp, mi = m // 2, m % 2
            if m == 12:
                for kq in range(4):
                    S.dma("pool", wo0[:, 4 * kq:4 * kq + 4, :], w_o_v[:, 4 * kq:4 * kq + 4, 0:512], "wo0", writes=[("wo0",)])
            wk = ("wmg%d" % (mp % 2),)
            slot = "wmg%d" % (mp % 2)
            if mi == 0:
                wb2 = wmg[mp % 2]
                c0 = 256 * mp
                S.dma("pool", wb2[:, 0:8, :], w_s_v[:, :, c0:c0 + 256], slot, writes=[wk])
                S.dma("pool", wb2[:, 8:16, :], w_c_v[:, :, c0:c0 + 256], slot, writes=[wk])
                S.dma("pool", wb2[:, 16:32, :], w_in_v[:, :, 4096 + c0:4096 + c0 + 256], slot, writes=[wk])
                S.dma("pool", wb2[:, 32:48, :], w_in_v[:, :, 6144 + c0:6144 + c0 + 256], slot, writes=[wk])
            wb = wb2[:, :, 128 * mi:128 * mi + 128]
            for blk in range(2):
                cs = slice(512 * blk, 512 * blk + 512)
                jobs = [(0, 8, g3, G3), (8, 8, cv, CV), (16, 16, hT, HT), (32, 16, hT, HT)]
                pbk = []
                for (w0_, nk, src, skeys) in jobs:
                    pb, pk = bank()
                    pbk.append((pb, pk))
                    for k in range(nk):
                        S.op("pe", lambda e, pb=pb, k=k, w0_=w0_, src=src, nk=nk, cs=cs, wb=wb: e.matmul(
                            pb[:, 0:512], lhsT=wb[:, w0_ + k, :], rhs=src[:, k, cs], start=(k == 0), stop=(k == nk - 1)),
                            reads=[wk] + skeys, writes=[pk])
                ek = lambda i: ("et", blk, i)
                (pys, kys), (pyc, kyc), (pgs, kgs), (pgc, kgc) = pbk
                S.op("act", lambda e, pys=pys, blk=blk: e.copy(out=et[:, blk, 0, :], in_=pys[:, 0:512]),
                     reads=[kys, ek(0)], writes=[ek(0)])
                S.op("act", lambda e, pyc=pyc, blk=blk: e.copy(out=et[:, blk, 1, :], in_=pyc[:, 0:512]),
                     reads=[kyc, ek(1)], writes=[ek(1)])
                S.op("act", lambda e, pgs=pgs, blk=blk, m=m: e.activation(
                    out=et[:, blk, 2, :], in_=pgs[:, 0:512], func=AF.Sigmoid, bias=vecs[:, V_GB + m:V_GB + m + 1]),
                    reads=[kgs, ek(2), "vecs"], writes=[ek(2)])
                S.op("act", lambda e, pgc=pgc, blk=blk, m=m: e.activation(
                    out=et[:, blk, 3, :], in_=pgc[:, 0:512], func=AF.Sigmoid, bias=vecs[:, V_GB + 16 + m:V_GB + 16 + m + 1]),
                    reads=[kgc, ek(3), "vecs"], writes=[ek(3)])
                tt("dve", et[:, blk, 0, :], et[:, blk, 0, :], et[:, blk, 2, :], ALU.mult, [ek(0), ek(2)], [ek(0)])
                tt("dve", et[:, blk, 1, :], et[:, blk, 1, :], et[:, blk, 3, :], ALU.mult, [ek(1), ek(3)], [ek(1)])
                tt("dve", mgd[:, m, cs], et[:, blk, 0, :], et[:, blk, 1, :], ALU.add, [ek(0), ek(1)], [("mg", m)])
        MG = [("mg", m) for m in range(16)]
        checkpoint("E0")
        dump("mg", mgd[:].rearrange("p a b -> p (a b)"), [128, 16 * NTOK], ("mg", 15), BF16)
        A.release(m_mid)

        osb = A.alloc("osb", [128, 8, D], F32)
        ssq = A.alloc("ssq", [128, 8, 4], F32)
        mF = A.mark()
        wo = [wo0, A.alloc("wo1", [128, 16, 512], BF16)]
        sqj = A.alloc("sqj", [128, 512], BF16)
        for nb in range(4):
            wb = wo[nb % 2]
            wk = ("wo%d" % (nb % 2),)
            for kq in range(4):
                if nb == 0:
                    break
                S.dma("pool", wb[:, 4 * kq:4 * kq + 4, :], w_o_v[:, 4 * kq:4 * kq + 4, 512 * nb:512 * nb + 512],
                      "wo%d" % (nb % 2), writes=[wk])
            for t in range(8):
                pb, pk = bank()
                for k in range(16):
                    S.op("pe", lambda e, pb=pb, k=k, t=t, wb=wb: e.matmul(
                        pb[:, 0:512], lhsT=mgd[:, k, 128 * t:128 * t + 128], rhs=wb[:, k, :],
                        start=(k == 0), stop=(k == 15)), reads=[wk] + MG, writes=[pk])
                S.op("act", lambda e, pb=pb, t=t, nb=nb: e.activation(
                    out=sqj[:], in_=pb[:, 0:512], func=AF.Square, accum_out=ssq[:, t, nb:nb + 1]),
                    reads=[pk, "sqj"], writes=["sqj", ("ssq", t)])
                S.op("dve", lambda e, pb=pb, t=t, nb=nb: e.tensor_copy(out=osb[:, t, 512 * nb:512 * nb + 512], in_=pb[:, 0:512]),
                     reads=[pk], writes=[("osb", t)])
        checkpoint("F")
        A.retire_fixed(["wo0"], TOPSLOT, TOPSLOT + 16384)
        A.release(mF)
        gpost = A.alloc("gpost", [128, 2, D], F32)
        S.dma("sp", gpost[:].rearrange("p a b -> p (a b)"), gpost_d, "gpost", writes=["gpost"])
        xin2 = [(A.alloc("xr0", [128, D], F32), ("xr0",)), (A.overlay("xr1", [128, D], F32, "gT", 0), ("xr1",))]
        xin3 = [(A.alloc("xs0", [128, D], F32), ("xs0",)), (A.overlay("xs1", [128, D], F32, "gT", 8192), ("xs1",))]
        sq2 = (A.alloc("sq2", [128, D], BF16), ("sq2",))
        sm2 = A.alloc("sm2", [128, 32], F32)

        def post_norm_residual(t, src, srck, ssum_ap, ssk, gi, res_d, res_rows, dstk):
            xb, xk = xin2[t % 2]
            S.dma("sp", xb[:], res_d[res_rows:res_rows + 128, :], xk[0], writes=[xk])
            sc = sm2[:, 2 * (t % 8):2 * (t % 8) + 1]
            sk = ("sm2", t % 8)
            S.op("dve", lambda e: e.tensor_reduce(out=sc, in_=ssum_ap, axis=mybir.AxisListType.X, op=ALU.add),
                 reads=[ssk], writes=[sk])
            S.op("act", lambda e: e.activation(out=sc, in_=sc, func=AF.Sqrt, bias=1e-6, scale=1.0 / D), reads=[sk], writes=[sk])
            S.op("dve", lambda e: e.reciprocal(out=sc, in_=sc), reads=[sk], writes=[sk])
            S.op("act", lambda e: e.activation(out=src, in_=src, func=AF.Copy, scale=sc), reads=[srck, sk], writes=[srck])
            tt("dve", src, src, gpost[:, gi, :], ALU.mult, [srck, "gpost"], [srck])
            tt("dve", src, src, xb[:], ALU.add, [srck, xk], [srck])

        sm4 = A.alloc("sm4", [128, 32], F32)
        for t in range(8):
            sc = sm2[:, t:t + 1]
            sk = ("sm2", t)
            S.op("dve", lambda e: e.tensor_reduce(out=sc, in_=ssq[:, t, :], axis=mybir.AxisListType.X, op=ALU.add),
                 reads=[("ssq", t)], writes=[sk])
            S.op("act", lambda e: e.activation(out=sc, in_=sc, func=AF.Sqrt, bias=1e-6, scale=1.0 / D), reads=[sk], writes=[sk])
            S.op("dve", lambda e: e.reciprocal(out=sc, in_=sc), reads=[sk], writes=[sk])

        def g_loadx(t):
            xb, xk = xin2[t % 2]
            S.dma("sp", xb[:], xseq[128 + 128 * t:256 + 128 * t, :], xk[0], writes=[xk])

        g_loadx(0)
        g_loadx(1)

        def g_B1(t):
            src, srck = osb[:, t, :], ("osb", t)
            xb, xk = xin2[t % 2]
            S.op("dve", lambda e: e.scalar_tensor_tensor(out=src, in0=src, scalar=sm2[:, t:t + 1], in1=gpost[:, 0, :],
                                                         op0=ALU.mult, op1=ALU.mult),
                 reads=[srck, ("sm2", t), "gpost"], writes=[srck])
            tt("dve", src, src, xb[:], ALU.add, [srck, xk], [srck])
            if t + 2 < 8:
                g_loadx(t + 2)
            S.dma("sp", out_d[128 * t:128 * t + 128, :], src, "hs_scr", reads=[srck], writes=[("hs_scr", t)])
            if t == 0:
                dump("hs2", osb[:, 0, :], [128, D], ("osb", 0))
            sc = sm4[:, t:t + 1]
            S.op("act", lambda e: e.activation(out=sq2[0][:], in_=src, func=AF.Square, accum_out=sc),
                 reads=[srck], writes=[sq2[1], ("sm4", t)])
            S.op("act", lambda e: e.activation(out=sc, in_=sc, func=AF.Sqrt, bias=1e-6, scale=1.0 / D),
                 reads=[("sm4", t)], writes=[("sm4", t)])

        def g_B2(t):
            src, srck = osb[:, t, :], ("osb", t)
            sc = sm4[:, t:t + 1]
            xs, xsk = xin3[t % 2]
            S.op("dve", lambda e: e.reciprocal(out=sc, in_=sc), reads=[("sm4", t)], writes=[("sm4", t)])
            S.op("act", lambda e: e.activation(out=xs[:], in_=src, func=AF.Copy, scale=sc),
                 reads=[srck, ("sm4", t)], writes=[xsk])

        def g_C(t):
            xs, xsk = xin3[t % 2]
            for q in range(4):
                pb, pk = bank()
                for kk in range(4):
                    k = 4 * q + kk
                    S.op("pe", lambda e: e.transpose(
                        out=pb[:, kk * 128:(kk + 1) * 128], in_=xs[:, k * 128:(k + 1) * 128], identity=ident),
                        reads=[xsk, "cF"], writes=[pk])
                g_b = mk(vecs[:, V_GFPRE + 4 * q:V_GFPRE + 4 * q + 4], [[1, 4], [0, 128]])
                S.op("dve", lambda e: e.tensor_tensor(
                    out=hT[:, 4 * q:4 * q + 4, 128 * t:128 * t + 128],
                    in0=pb[:].rearrange("p (a b) -> p a b", a=4), in1=g_b, op=ALU.mult),
                    reads=[pk, "vecs"], writes=[("hT", t // 4)])

        for it in range(10):
            if it < 8:
                g_B1(it)
            if 1 <= it < 9:
                g_B2(it - 1)
            if it >= 2:
                g_C(it - 2)

        checkpoint("G")
        A.release(m_main)
        facc = A.alloc("facc", [128, 8, D], F32)
        agr = A.alloc("agr", [128, 11, NTOK], BF16)
        WB = A.alloc("WB", [128, 4 * 5632], BF16)
        wfo4 = [WB[:, 5632 * i:5632 * (i + 1)].rearrange("p (k n) -> p k n", k=11) for i in range(4)]
        wfo = wfo4[0:2]
        wfi = [WB[:, 11264 + 4096 * i:11264 + 4096 * (i + 1)].rearrange("p (s k n) -> p s k n", s=2, k=16) for i in range(2)]
        for nm_ in ("wfi0", "wfi1", "wfo0", "wfo1", "wfoL2", "wfoL3"):
            S.inherit[nm_] = S.inherit.get("WB", [])
        sgt = A.alloc("sgt", [128, 2, 512], F32)
        gpost = A.alloc("gpostb", [128, 2, D // 2], F32)
        gpost_f = gpost[:].rearrange("p a b -> p (a b)")
        S.dma("sp", gpost_f, gpost_d[:, D:2 * D], "gpostb", writes=["gpostb"])
        wfi.append(A.alloc("wfi2", [128, 2, 16, 128], BF16)[:])
        xin2 = [A.alloc("xq%d" % i, [128, D], F32) for i in range(2)]
        ssq2 = A.alloc("ssq2", [128, 8, 4], F32)
        sqj2 = A.alloc("sqj2", [128, 512], BF16)
        sm3 = A.alloc("sm3", [128, 32], F32)
        wfi_ctr = 0
        wfo_ctr = 0

        def final_stats(t):
            sc = sm3[:, 2 * t:2 * t + 1]
            sk = ("sm3", t)
            xb = xin2[t % 2]
            xk = ("xq%d" % (t % 2),)
            S.dma("sp", xb[:], out_d[128 * t:128 * t + 128, :], "xq%d" % (t % 2), reads=[("hs_scr", t)], writes=[xk])
            S.op("dve", lambda e: e.tensor_reduce(out=sc, in_=ssq2[:, t, :], axis=mybir.AxisListType.X, op=ALU.add),
                 reads=[("ssq2", t, nb) for nb in range(4)], writes=[sk])
            S.op("act", lambda e: e.activation(out=sc, in_=sc, func=AF.Sqrt, bias=1e-6, scale=1.0 / D), reads=[sk], writes=[sk])
            S.op("dve", lambda e: e.reciprocal(out=sc, in_=sc), reads=[sk], writes=[sk])

        def final_heavy(t):
            src = facc[:, t, :]
            fks = [("facc", t, nb) for nb in range(4)]
            xb = xin2[t % 2]
            xk = ("xq%d" % (t % 2),)
            ok = ("fo", t)
            S.op("dve", lambda e: e.scalar_tensor_tensor(out=src, in0=src, scalar=sm3[:, 2 * t:2 * t + 1], in1=gpost_f,
                                                         op0=ALU.mult, op1=ALU.mult),
                 reads=fks + [("sm3", t), "gpostb"], writes=[ok])
            tt("dve", src, src, xb[:], ALU.add, [ok, xk], [ok])
            S.dma("sp", out_d[128 * t:128 * t + 128, :], src, "outst", reads=[ok, xk], writes=[("hs_scr", t)], is_output=True)

        for grp in range(4):
            for mi in range(11):
                mch = 11 * grp + mi
                wb = wfi[wfi_ctr % 3]
                wk = ("wfi%d" % (wfi_ctr % 3),)
                slot = "wfi%d" % (wfi_ctr % 3)
                wfi_ctr += 1
                S.dma("pool", wb[:, 0, :, :], w_fi_v[:, :, 128 * mch:128 * mch + 128], slot, writes=[wk])
                S.dma("pool", wb[:, 1, :, :], w_fi_v[:, :, FFN + 128 * mch:FFN + 128 * mch + 128], slot, writes=[wk])
                for blk in range(2):
                    cs = slice(512 * blk, 512 * blk + 512)
                    pg, kg = bank()
                    pu, ku = bank()
                    for (pb, pk, s) in ((pg, kg, 0), (pu, ku, 1)):
                        for k in range(16):
                            S.op("pe", lambda e, pb=pb, s=s, k=k, cs=cs, wb=wb: e.matmul(
                                pb[:, 0:512], lhsT=wb[:, s, k, :], rhs=hT[:, k, cs], start=(k == 0), stop=(k == 15)),
                                reads=[wk, ("hT", 0), ("hT", 1)], writes=[pk])
                    S.op("act", lambda e, pg=pg, blk=blk: e.activation(out=sgt[:, blk, :], in_=pg[:, 0:512], func=AF.Silu),
                         reads=[kg, ("sgt", blk)], writes=[("sgt", blk)])
                    S.op("dve", lambda e, pu=pu, blk=blk, mi=mi, cs=cs: e.tensor_tensor(
                        out=agr[:, mi, cs], in0=pu[:, 0:512], in1=sgt[:, blk, :], op=ALU.mult),
                        reads=[ku, ("sgt", blk), ("agr", mi)], writes=[("agr", mi)])
            AG = [("agr", i) for i in range(11)]

            def fo_tile(t, nb, wb, wk):
                pb, pk = bank()
                for k in range(11):
                    S.op("pe", lambda e: e.matmul(
                        pb[:, 0:512], lhsT=agr[:, k, 128 * t:128 * t + 128], rhs=wb[:, k, :],
                        start=(k == 0), stop=(k == 10)), reads=[wk] + AG, writes=[pk])
                dst = facc[:, t, 512 * nb:512 * nb + 512]
                fk = ("facc", t, nb)
                if grp == 0:
                    S.op("act", lambda e: e.copy(out=dst, in_=pb[:, 0:512]), reads=[pk], writes=[fk])
                else:
                    S.op("dve", lambda e: e.tensor_tensor(out=dst, in0=dst, in1=pb[:, 0:512], op=ALU.add),
                         reads=[pk, fk], writes=[fk])
                if grp == 3:
                    S.op("act", lambda e: e.activation(
                        out=sqj2[:], in_=dst, func=AF.Square, accum_out=ssq2[:, t, nb:nb + 1]),
                        reads=[fk, "sqj2"], writes=["sqj2", ("ssq2", t, nb)])

            if grp < 3:
                for nb in range(4):
                    wb = wfo[wfo_ctr % 2]
                    wk = ("wfo%d" % (wfo_ctr % 2),)
                    slot = "wfo%d" % (wfo_ctr % 2)
                    wfo_ctr += 1
                    S.dma("pool", wb[:, :, :], w_fo_v[:, 11 * grp:11 * grp + 11, 512 * nb:512 * nb + 512], slot, writes=[wk])
                    for t in range(8):
                        fo_tile(t, nb, wb, wk)
            else:
                wks = [("wfo0",), ("wfo1",), ("wfoL2",), ("wfoL3",)]
                extra = [[], [], [("wfi0",), ("wfi1",)], [("wfi1",)]]
                for nb in range(4):
                    S.dma("pool", wfo4[nb][:, :, :], w_fo_v[:, 11 * grp:11 * grp + 11, 512 * nb:512 * nb + 512],
                          wks[nb][0], writes=[wks[nb]] + extra[nb])
                for t in range(8):
                    for nb in range(4):
                        fo_tile(t, nb, wfo4[nb], wks[nb])
                    final_stats(t)
                    if t > 0:
                        final_heavy(t - 1)
                final_heavy(7)

def _consts():
    cF = np.zeros((128, 768), np.float32)
    cF[:, 0:128] = np.eye(128, dtype=np.float32)
    r = np.arange(128)
    jj, ii = r[:, None] // 16, r[None, :] // 16
    cF[:, 128:256] = (ii >= jj).astype(np.float32)
    cF[:, 256:384] = (jj >= ii).astype(np.float32)
    cF[:, 384:448] = np.concatenate([np.eye(64), np.eye(64)], 0).astype(np.float32)
    cF[:, 448:465] = np.array([1, 2, 3, 4, 5, 6, 7, 8, -1, -2, -3, -4, -5, -6, -7, -8, 8], np.float32)[None, :]
    cF[:, 480:768] = np.arange(288, dtype=np.float32)[None, :]
    strips = np.zeros((128, 8, 240), np.float32)
    for x in range(8):
        for c in range(16):
            strips[16 * x + c, x, 7 * 16 + c] = 1.0
    return cF, strips.reshape(128, 1920).astype(ml_dtypes.bfloat16)


def _core_inputs(inp, b, half, shared):
    x = np.asarray(inp["x"])[b]
    meta = np.asarray(inp["meta"])
    z112 = np.zeros((112, D), np.float32)
    z128 = np.zeros((128, D), np.float32)
    if half == 0:
        xseq = np.concatenate([z112, meta, x[:1024], x[1024:], z128], 0)
        od = (0, 1)
    else:
        xseq = np.concatenate([z128, x[1024:][::-1], x[:1024][::-1], meta[::-1], z112], 0)
        od = (1, 0)
    pA = np.zeros((128, 4288), np.float32)
    lam_re, lam_im, log_dt = inp["lam_re"][0], inp["lam_im"][0], inp["log_dt"][0]
    b_re, b_im, c_re, c_im = inp["b_re"][0], inp["b_im"][0], inp["c_re"][0], inp["c_im"][0]
    for d in range(2):
        o = od[d]
        for b8 in range(8):
            for gq in range(4):
                n = d * 32 + b8 * 4 + gq
                for hh in range(2):
                    g = 8 * b8 + 4 * hh + gq
                    rs = slice(64 * hh, 64 * hh + 64)
                    pA[rs, 0 + n] = lam_re[o, g]
                    pA[rs, 64 + n] = lam_im[o, g]
                    pA[rs, 128 + n] = log_dt[o, g]
                    pA[rs, 192 + 16 * n:192 + 16 * n + 16] = b_re[o, g]
                    pA[rs, 1216 + 16 * n:1216 + 16 * n + 16] = b_im[o, g]
                    pA[rs, 2240 + 16 * n:2240 + 16 * n + 16] = c_re[o, g].T
                    pA[rs, 3264 + 16 * n:3264 + 16 * n + 16] = c_im[o, g].T
    vecs = np.zeros((128, 128), np.float32)

    def fm(v):
        return np.asarray(v, np.float32).reshape(-1, 128).T

    vecs[:, 0:16] = fm(inp["g_mix_pre"][0])
    vecs[:, 16:32] = fm(inp["g_ffn_pre"][0])
    vecs[:, 32:64] = fm(inp["gate_b"][0])
    vecs[:, 64:72] = fm(inp["d_skip"][0])
    vecs[:, 72:80] = fm(inp["b_glu"][0])
    cw = np.asarray(inp["conv_w"][0])
    if half == 1:
        cw = cw[::-1]
    vecs[:, 80:88] = fm(cw[0])
    vecs[:, 88:96] = fm(cw[1])
    vecs[:, 96:104] = fm(cw[2])
    vecs[:, 104:112] = fm(inp["conv_b"][0])
    m = dict(shared)
    m.update({"xseq": np.ascontiguousarray(xseq), "pA": pA, "vecs": vecs})
    return m


_NC_CACHE = {}


def kernel(**inputs):
    inp = {k: np.asarray(v) for k, v in inputs.items()}
    cF, strips = _consts()
    gpost = np.concatenate([np.broadcast_to(inp["g_mix_post"][0][None, :], (128, D)),
                            np.broadcast_to(inp["g_ffn_post"][0][None, :], (128, D))], 1).astype(np.float32)
    shared = {
        "gpost": np.ascontiguousarray(gpost), "cF": cF, "strips": strips,
        "w_in": np.ascontiguousarray(inp["w_in"][0]), "w_glu": np.ascontiguousarray(inp["w_glu"][0]),
        "w_s_up": np.ascontiguousarray(inp["w_s_up"][0]), "w_c_up": np.ascontiguousarray(inp["w_c_up"][0]),
        "w_o": np.ascontiguousarray(inp["w_o"][0]), "w_ffn_in": np.ascontiguousarray(inp["w_ffn_in"][0]),
        "w_ffn_out": np.ascontiguousarray(inp["w_ffn_out"][0]),
    }
    in_maps = [_core_inputs(inp, c // 2, c % 2, shared) for c in range(8)]
    if "nc" not in _NC_CACHE:
        _NC_CACHE["nc"] = build_nc(DEBUG)
    nc, dbg = _NC_CACHE["nc"]
    ncores = NCORES
    res = run_bass_kernel_spmd(nc, in_maps[:ncores], core_ids=list(range(ncores)))
    out = np.zeros((4, 2048, D), np.float32)
    for c in range(ncores):
        o = np.asarray(res.results[c]["out"])
        if c % 2 == 0:
            out[c // 2, :1024] = o
        else:
            out[c // 2, 1024:] = o[::-1]
    if DEBUG:
        kernel.last_results = res.results
    return out
```

```python
import math
from contextlib import ExitStack

import numpy as np
import ml_dtypes

import concourse.bass as bass
import concourse.mybir as mybir
from concourse.ap import AP
from concourse.bass_utils import run_bass_kernel_spmd

F32 = mybir.dt.float32
BF16 = mybir.dt.bfloat16
ALU = mybir.AluOpType
AF = mybir.ActivationFunctionType

D = 2048
NTOK = 1024
NSEQ = 2304
NCH = NSEQ // 8
FFN = 5632
TWO_PI = 2.0 * math.pi
DEBUG = False
STOP = None
NCORES = 8


class _Stop(Exception):
    pass


class Sched:
    def __init__(self, nc, stack, n_dma_sems=72):
        self.nc = nc
        self.eng = {"pe": nc.tensor, "act": nc.scalar, "dve": nc.vector,
                    "pool": nc.gpsimd, "sp": nc.sync}
        self.sem = {k: stack.enter_context(nc.semaphore("s_" + k)) for k in self.eng}
        self.cnt = {k: 0 for k in self.eng}
        self.known = {k: {} for k in self.eng}
        self.free_dma = [stack.enter_context(nc.semaphore("d%d" % i)) for i in range(n_dma_sems)]
        self.dma_sem = {}
        self.dma_cnt = {}
        self.last_w = {}
        self.readers = {}
        self.inherit = {}
        self.out_events = []

    def _wait(self, e, ev):
        src, val = ev
        if src == e and e == "pe":
            return
        kn = self.known[e]
        if kn.get(src, 0) >= val:
            return
        if not isinstance(src, str):
            val = self.dma_cnt[src[1]]
        kn[src] = val
        sem = self.sem[src] if isinstance(src, str) else self.dma_sem[src[1]]
        self.eng[e].wait_ge(sem, val)

    def _deps(self, e, reads, writes):
        evs = []
        for k in reads:
            w = self.last_w.get(k)
            if w is not None:
                evs.append(w)
            if isinstance(k, tuple) and k[0] == "ps":
                for s_, v_ in self.readers.get(k, {}).items():
                    if s_ != e:
                        evs.append((s_, v_))
        for k in writes:
            w = self.last_w.get(k)
            if w is not None:
                evs.append(w)
            else:
                tn = k[0] if isinstance(k, tuple) else k
                evs.extend(self.inherit.get(tn, []))
            evs.extend(self.readers.get(k, {}).items())
        for ev in evs:
            self._wait(e, ev)

    def _record(self, ev, reads, writes):
        for k in reads:
            r = self.readers.setdefault(k, {})
            if r.get(ev[0], 0) < ev[1]:
                r[ev[0]] = ev[1]
        for k in writes:
            self.last_w[k] = ev
            self.readers[k] = {}

    def op(self, e, fn, reads=(), writes=()):
        self._deps(e, reads, writes)
        ins = fn(self.eng[e])
        self.cnt[e] += 1
        ins.then_inc(self.sem[e], 1)
        ev = (e, self.cnt[e])
        self._record(ev, reads, writes)
        return ev

    def dma(self, e, out, in_, slot, reads=(), writes=(), is_output=False):
        self._deps(e, reads, writes)
        if slot not in self.dma_sem:
            self.dma_sem[slot] = self.free_dma.pop()
            self.dma_cnt[slot] = 0
        self.dma_cnt[slot] += 16
        self.eng[e].dma_start(out=out, in_=in_).then_inc(self.dma_sem[slot], 16)
        ev = (("dma", slot), self.dma_cnt[slot])
        self._record(ev, reads, writes)
        if is_output:
            self.out_events.append(ev)
        return ev

    def all_events(self, tnames):
        evs = {}
        for table in (self.last_w,):
            for k, ev in table.items():
                tn = k[0] if isinstance(k, tuple) else k
                if tn in tnames and evs.get(ev[0], 0) < ev[1]:
                    evs[ev[0]] = ev[1]
        for k, r in self.readers.items():
            tn = k[0] if isinstance(k, tuple) else k
            if tn in tnames:
                for s, v in r.items():
                    if evs.get(s, 0) < v:
                        evs[s] = v
        for tn in tnames:
            for s, v in self.inherit.get(tn, []):
                if evs.get(s, 0) < v:
                    evs[s] = v
        return list(evs.items())

    def finish(self, e="sp"):
        for ev in self.out_events:
            self._wait(e, ev)


class Arena:
    LO = 16512
    HI = 229376

    def __init__(self, nc, S):
        self.nc, self.S = nc, S
        self.top = self.LO
        self.live = []
        self.dead = []
        self.uid = 0
        self.addr = {}
        self.peak = 0

    def overlay(self, name, shape, dtype, base, off=0):
        self.uid += 1
        t = self.nc.alloc_sbuf_tensor_at("%s_%d" % (name, self.uid), list(shape), dtype, offset=self.addr[base] + off)
        self.S.inherit[name] = self.S.all_events({base})
        return t

    def alloc(self, name, shape, dtype):
        nbytes = int(np.prod(shape[1:])) * (2 if dtype == BF16 else 4)
        lo = (self.top + 63) // 64 * 64
        hi = lo + nbytes
        assert hi <= self.HI, ("SBUF overflow", name, hi)
        assert getattr(self, "fixed_lo", None) is None or hi <= self.fixed_lo, ("stack runs into fixed slot", name, hi)
        self.top = hi
        self.uid += 1
        t = self.nc.alloc_sbuf_tensor_at("%s_%d" % (name, self.uid), list(shape), dtype, offset=lo)
        self.live.append((name, lo, hi))
        self.addr[name] = lo
        self.peak = max(self.peak, hi)
        evs = []
        for (dlo, dhi, devs) in self.dead:
            if dlo < hi and lo < dhi:
                evs.extend(devs)
        if evs:
            self.S.inherit[name] = evs
        return t

    def fixed(self, name, shape, dtype, lo):
        nbytes = int(np.prod(shape[1:])) * (2 if dtype == BF16 else 4)
        hi = lo + nbytes
        assert hi <= self.HI and lo >= self.top, ("fixed slot collides with the stack", name, lo, self.top)
        self.uid += 1
        t = self.nc.alloc_sbuf_tensor_at("%s_%d" % (name, self.uid), list(shape), dtype, offset=lo)
        evs = []
        for (dlo, dhi, devs) in self.dead:
            if dlo < hi and lo < dhi:
                evs.extend(devs)
        self.S.inherit[name] = evs
        self.fixed_lo = lo
        return t

    def retire_fixed(self, names, lo, hi):
        self.dead.append((lo, hi, self.S.all_events(set(names))))
        self.fixed_lo = None

    def mark(self):
        return (self.top, len(self.live))

    def release(self, mark):
        top, n = mark
        gone = self.live[n:]
        self.live = self.live[:n]
        if gone:
            names = set(g[0] for g in gone)
            evs = self.S.all_events(names)
            lo = min(g[1] for g in gone)
            hi = max(g[2] for g in gone)
            keep = []
            for dd in self.dead:
                if dd[0] >= lo and dd[1] <= hi:
                    evs = evs + dd[2]
                else:
                    keep.append(dd)
            self.dead = keep + [(lo, hi, evs)]
        self.top = top


def mk(v, dims, off=0):
    return AP(v.tensor, v.offset + off, [list(v.ap[0])] + [list(d) for d in dims])


def build_nc(debug=False):
    nc = bass.Bass("TRN2", target_bir_lowering=False)

    def din(name, shape, dt=F32):
        return nc.dram_tensor(name, list(shape), dt, kind="ExternalInput").ap()

    xseq = din("xseq", [NSEQ, D])
    pA_d = din("pA", [128, 4288])
    vecs_d = din("vecs", [128, 128])
    gpost_d = din("gpost", [128, 2 * D])
    cF_d = din("cF", [128, 768])
    strips_d = din("strips", [128, 1920], BF16)
    w_in = din("w_in", [D, 8192])
    w_glu = din("w_glu", [1024, 1024])
    w_s_up = din("w_s_up", [1024, D])
    w_c_up = din("w_c_up", [1024, D])
    w_o = din("w_o", [D, D])
    w_fi = din("w_ffn_in", [D, 2 * FFN])
    w_fo = din("w_ffn_out", [FFN, D])
    out_d = nc.dram_tensor("out", [NTOK, D], F32, kind="ExternalOutput").ap()
    hs_scr = None
    dbg = {}

    w_in_v = w_in.rearrange("(k p) n -> p k n", p=128)
    w_glu_v = w_glu.rearrange("(k p) n -> p k n", p=128)
    w_s_v = w_s_up.rearrange("(k p) n -> p k n", p=128)
    w_c_v = w_c_up.rearrange("(k p) n -> p k n", p=128)
    w_o_v = w_o.rearrange("(k p) n -> p k n", p=128)
    w_fi_v = w_fi.rearrange("(k p) n -> p k n", p=128)
    w_fo_v = w_fo.rearrange("(k p) n -> p k n", p=128)

    with ExitStack() as stack:
        S = Sched(nc, stack)
        try:
            _program(nc, S, locals())
        except _Stop:
            pass
        S.finish("sp")
    return nc, dbg


def _program(nc, S, env):
    globals_ = env
    xseq, pA_d, vecs_d, gpost_d, cF_d, strips_d = (env[k] for k in ("xseq", "pA_d", "vecs_d", "gpost_d", "cF_d", "strips_d"))
    w_in_v, w_glu_v, w_s_v, w_c_v, w_o_v, w_fi_v, w_fo_v = (env[k] for k in ("w_in_v", "w_glu_v", "w_s_v", "w_c_v", "w_o_v", "w_fi_v", "w_fo_v"))
    out_d, hs_scr, dbg, debug = env["out_d"], env["hs_scr"], env["dbg"], env["debug"]

    pend = [False]

    def checkpoint(name):
        if STOP == name:
            if name in ("A0", "C0", "E0"):
                pend[0] = True
            else:
                raise _Stop()

    if True:
        A = Arena(nc, S)
        ps = [nc.alloc_psum_tensor("ps%d" % i, [128, 512], F32) for i in range(8)]
        ps_rr = [0]

        ps_pool = [list(range(8))]

        def bank():
            pool = ps_pool[0]
            i = pool[ps_rr[0] % len(pool)]
            ps_rr[0] += 1
            return ps[i], ("ps", i)

        def dump(name, src_ap, shape, key, dt=F32):
            if not debug:
                return
            d = nc.dram_tensor("dbg_" + name, list(shape), dt, kind="ExternalOutput").ap()
            dbg[name] = d
            S.dma("sp", d, src_ap, "dbg_" + name, reads=[key], is_output=True)
            if pend[0]:
                raise _Stop()

        cF = A.alloc("cF", [128, 768], F32)
        strips = A.alloc("strips", [128, 8, 240], BF16)
        vecs = A.alloc("vecs", [128, 128], F32)
        S.dma("sp", cF[:], cF_d, "c0", writes=["cF"])
        S.dma("sp", strips[:].rearrange("p a b -> p (a b)"), strips_d, "c1", writes=["strips"])
        S.dma("sp", vecs[:], vecs_d, "c2", writes=["vecs"])
        ident = cF[:, 0:128]
        maskA = cF[:, 128:256]
        maskB = cF[:, 256:384]
        I2 = cF[:, 384:448]
        mvals = cF[:, 448:465]
        kpos = cF[:, 480:768]
        V_GPRE, V_GFPRE, V_GB, V_DSK, V_BGLU, V_CW, V_CB = 0, 16, 32, 64, 72, 80, 104

        hT = A.alloc("hT", [128, 16, 1026], BF16)
        small = A.alloc("small", [128, 64], F32)
        m_main = A.mark()

        gT = A.alloc("gT", [128, 8, NTOK], BF16)
        m1 = A.mark()
        u_fm = A.alloc("u_fm", [128, 8, NSEQ], BF16)
        m2 = A.mark()
        xin = [(A.alloc("xin%d" % i, [128, D], F32), ("xin%d" % i,)) for i in range(3)]
        sq = (A.alloc("sq", [128, D], BF16), ("sq",))
        hTts = [A.alloc("hTt%d" % i, [128, 16, 512], BF16) for i in range(2)]
        wus = A.alloc("wus", [128, 16, 1024], BF16)

        for kq in range(4):
            S.dma("pool", wus[:, 4 * kq:4 * kq + 4, :], w_in_v[:, 4 * kq:4 * kq + 4, 0:1024],
                  "wus", writes=[("wus", kq)])

        tile_ctr = [0]

        def norm_tile(row0, dst, dst_col, gcol, src_d=xseq, src_sb=None, dst_key=None, bufs=None, sqb=None):
            i = tile_ctr[0]
            tile_ctr[0] += 1
            bufs = bufs or xin
            sqt, sqk = sqb or sq
            xb, xk = bufs[i % 2]
            sc = small[:, (i % 8) * 2:(i % 8) * 2 + 1]
            sk = ("small", i % 8)
            if src_sb is None:
                S.dma("sp", xb[:], src_d[row0:row0 + 128, :], xk[0], writes=[xk])
                src, srck = xb[:], xk
            else:
                src, srck = src_sb
            S.op("act", lambda e: e.activation(out=sqt[:], in_=src, func=AF.Square, accum_out=sc),
                 reads=[srck], writes=[sqk, sk])
            S.op("act", lambda e: e.activation(out=sc, in_=sc, func=AF.Sqrt, bias=1e-6, scale=1.0 / D),
                 reads=[sk], writes=[sk])
            S.op("dve", lambda e: e.reciprocal(out=sc, in_=sc), reads=[sk], writes=[sk])
            S.op("act", lambda e: e.activation(out=xb[:], in_=src, func=AF.Copy, scale=sc),
                 reads=[srck, sk], writes=[xk])
            for q in range(4):
                pb, pk = bank()
                for kk in range(4):
                    k = 4 * q + kk
                    S.op("pe", lambda e, kk=kk, k=k: e.transpose(
                        out=pb[:, kk * 128:(kk + 1) * 128], in_=xb[:, k * 128:(k + 1) * 128], identity=ident),
                        reads=[xk, "cF"], writes=[pk])
                gv = vecs[:, gcol + 4 * q:gcol + 4 * q + 4]
                g_b = mk(gv, [[1, 4], [0, 128]])
                eng = "dve" if q % 2 == 0 else "pool"
                eng = "dve"
                S.op(eng, lambda e, q=q, g_b=g_b, pb=pb: e.tensor_tensor(
                    out=dst[:, 4 * q:4 * q + 4, dst_col:dst_col + 128],
                    in0=pb[:].rearrange("p (a b) -> p a b", a=4), in1=g_b, op=ALU.mult),
                    reads=[pk, "vecs"], writes=[dst_key])

        def usproj(src, src_key, ncols, ucol0):
            for m in range(8):
                pb, pk = bank()
                for k in range(16):
                    S.op("pe", lambda e, k=k, m=m: e.matmul(
                        pb[:, 0:ncols], lhsT=wus[:, k, m * 128:(m + 1) * 128], rhs=src[:, k, 0:ncols],
                        start=(k == 0), stop=(k == 15)),
                        reads=[("wus", k // 4), src_key], writes=[pk])
                S.op("act", lambda e, m=m, pb=pb: e.copy(
                    out=mk(u_fm[:, m, :], [[NCH, 8], [1, ncols // 8]], off=ucol0 // 8),
                    in_=mk(pb[:, 0:ncols], [[1, 8], [8, ncols // 8]])),
                     reads=[pk], writes=[("u_fm", m)])

        def norm_stats(i, row0):
            xb, xk = xin[i % 3]
            sqt, sqk = sq
            sc = small[:, (i % 8) * 2:(i % 8) * 2 + 1]
            sk = ("small", i % 8)
            S.dma("sp", xb[:], xseq[row0:row0 + 128, :], xk[0], writes=[xk])
            S.op("act", lambda e: e.activation(out=sqt[:], in_=xb[:], func=AF.Square, accum_out=sc),
                 reads=[xk], writes=[sqk, sk])
            S.op("act", lambda e: e.activation(out=sc, in_=sc, func=AF.Sqrt, bias=1e-6, scale=1.0 / D),
                 reads=[sk], writes=[sk])
            S.op("dve", lambda e: e.reciprocal(out=sc, in_=sc), reads=[sk], writes=[sk])
            S.op("act", lambda e: e.activation(out=xb[:], in_=xb[:], func=AF.Copy, scale=sc),
                 reads=[xk, sk], writes=[xk])

        def norm_trans(i, dst, dst_col, gcol, dst_key):
            xb, xk = xin[i % 3]
            for q in range(4):
                pb, pk = bank()
                for kk in range(4):
                    k = 4 * q + kk
                    S.op("pe", lambda e: e.transpose(
                        out=pb[:, kk * 128:(kk + 1) * 128], in_=xb[:, k * 128:(k + 1) * 128], identity=ident),
                        reads=[xk, "cF"], writes=[pk])
                g_b = mk(vecs[:, gcol + 4 * q:gcol + 4 * q + 4], [[1, 4], [0, 128]])
                S.op("dve", lambda e: e.tensor_tensor(
                    out=dst[:, 4 * q:4 * q + 4, dst_col:dst_col + 128],
                    in0=pb[:].rearrange("p (a b) -> p a b", a=4), in1=g_b, op=ALU.mult),
                    reads=[pk, "vecs"], writes=[dst_key])

        HKS = [("hTt0",), ("hTt1",)]
        tiles = [(128 + 128 * t, hT, 128 * t, ("hT", t // 4)) for t in range(8)]
        tiles += [(1152 + 128 * t, hTts[t // 4], 128 * (t % 4), HKS[t // 4]) for t in range(8)]
        tiles += [(0, hTts[0], 0, HKS[0]), (2176, hTts[0], 128, HKS[0])]

        def after_tile(t):
            if t == 7:
                for blk in range(2):
                    usproj(hT[:, :, 512 * blk:512 * blk + 512], ("hT", blk), 512, 128 + 512 * blk)
            if t == 11:
                S.op("dve", lambda e: e.tensor_copy(out=hT[:, :, 1025:1026], in_=hTts[0][:, :, 0:1]),
                     reads=[HKS[0]], writes=[("hT", 2)])
                usproj(hTts[0], HKS[0], 512, 1152)
            if t == 15:
                usproj(hTts[1], HKS[1], 512, 1664)
            if t == 17:
                hTt = hTts[0]
                S.op("dve", lambda e: e.tensor_copy(out=hT[:, :, 1024:1025], in_=hTt[:, :, 127:128]),
                     reads=[HKS[0]], writes=[("hT", 2)])
                for m in range(8):
                    pb, pk = bank()
                    for k in range(16):
                        S.op("pe", lambda e: e.matmul(
                            pb[:, 0:256], lhsT=wus[:, k, m * 128:(m + 1) * 128], rhs=hTt[:, k, 0:256],
                            start=(k == 0), stop=(k == 15)),
                            reads=[("wus", k // 4), HKS[0]], writes=[pk])
                    S.op("act", lambda e: e.copy(
                        out=mk(u_fm[:, m, :], [[NCH, 8], [1, 16]], off=0),
                        in_=mk(pb[:, 0:128], [[1, 8], [8, 16]])),
                         reads=[pk], writes=[("u_fm", m)])
                    S.op("act", lambda e: e.copy(
                        out=mk(u_fm[:, m, :], [[NCH, 8], [1, 16]], off=272),
                        in_=mk(pb[:, 0:256], [[1, 8], [8, 16]], off=128)),
                         reads=[pk], writes=[("u_fm", m)])

        norm_stats(0, tiles[0][0])
        norm_stats(1, tiles[1][0])
        for t in range(len(tiles)):
            if t + 2 < len(tiles):
                norm_stats(t + 2, tiles[t + 2][0])
            row0, dst, col, key = tiles[t]
            norm_trans(t, dst, col, V_GPRE, key)
            after_tile(t)
        checkpoint("A0")
        dump("u", u_fm[:].rearrange("p a b -> p (a b)"), [128, 8 * NSEQ], ("u_fm", 7), BF16)

        checkpoint("A")
        A.release(m2)
        pA = A.alloc("pA", [128, 4288], F32)
        S.dma("sp", pA[:], pA_d, "pA", writes=["pA"])
        LRE, LIM, LDT, BR, BI, CR, CI = 0, 64, 128, 192, 1216, 2240, 3264
        tbK = A.alloc("tbK", [128, 2, 17 * 64], F32)
        sm = A.alloc("sm", [128, 24, 64], F32)
        wtab = A.alloc("wtab", [128, 2, 8 * 64], F32)
        mt = A.mark()
        tbT = A.alloc("tbT", [128, 6, 17 * 64], F32)
        T_ARG, T_E, T_S, T_C, T_T1, T_T2, T_ARE, T_AIM = range(8)

        def tsl(i):
            return tbT[:, i, :] if i < 6 else tbK[:, i - 6, :]

        def TK(i):
            return ("tbT", i) if i < 6 else ("tbK", i - 6)

        def tbv(i, m0=0, m1=17):
            return tsl(i)[:, m0 * 64:m1 * 64]

        (s_lr, s_li, s_dt, s_ldt, s_th, s_zr, s_zi, s_den, s_t1, s_t2, s_t3, s_rho, s_phi,
         s_c8, s_s8, s_ns8) = range(16)
        def tt(eng, out, a, b, op, reads, writes):
            S.op(eng, lambda e: e.tensor_tensor(out=out, in0=a, in1=b, op=op), reads=reads, writes=writes)

        I32 = mybir.dt.int32
        halfpi = small[:, 33:34]
        S.op("dve", lambda e: e.memset(halfpi, math.pi / 2), writes=[("small", 98)])

        def range_reduce(r, x, qi, qf, xk, rk, qik, qfk):
            S.op("dve", lambda e: e.tensor_scalar(out=qi, in0=x, scalar1=1.0 / TWO_PI, scalar2=None, op0=ALU.mult),
                 reads=[xk, qik], writes=[qik])
            S.op("dve", lambda e: e.scalar_tensor_tensor(out=r, in0=qi, scalar=-TWO_PI, in1=x, op0=ALU.mult, op1=ALU.add),
                 reads=[qik, xk, rk], writes=[rk])

        SIN_S = 1.0 - 1e-5

        def sincos(sn, co, r, tmp, rk, snk, cok, tmpk):
            S.op("act", lambda e: e.activation(out=sn, in_=r, func=AF.Sin, scale=SIN_S), reads=[rk, snk], writes=[snk])
            S.op("act", lambda e: e.activation(out=tmp, in_=r, func=AF.Abs), reads=[rk, tmpk], writes=[tmpk])
            S.op("act", lambda e: e.activation(out=co, in_=tmp, func=AF.Sin, bias=halfpi, scale=-SIN_S),
                 reads=[tmpk, cok, ("small", 98)], writes=[cok])

        K = "sm"
        S.op("dve", lambda e: e.tensor_scalar(out=sm[:, s_lr, :], in0=pA[:, LRE:LRE + 64], scalar1=-1e-4,
                                              scalar2=None, op0=ALU.min), reads=["pA"], writes=[K])
        S.op("act", lambda e: e.activation(out=sm[:, s_dt, :], in_=pA[:, LDT:LDT + 64], func=AF.Exp),
             reads=["pA"], writes=[K])
        tt("dve", sm[:, s_ldt, :], sm[:, s_lr, :], sm[:, s_dt, :], ALU.mult, [K], [K])
        tt("dve", sm[:, s_th, :], pA[:, LIM:LIM + 64], sm[:, s_dt, :], ALU.mult, [K, "pA"], [K])
        mv_b = mk(mvals, [[1, 17], [0, 64]])
        ldt_b = mk(sm[:, s_ldt, :], [[0, 17], [1, 64]])
        th_b = mk(sm[:, s_th, :], [[0, 17], [1, 64]])
        v3 = lambda i: tsl(i).rearrange("p (m n) -> p m n", m=17)
        tt("dve", v3(T_ARG), mv_b, ldt_b, ALU.mult, [K, "cF"], [TK(T_ARG)])
        S.op("act", lambda e: e.activation(out=tbv(T_E), in_=tbv(T_ARG), func=AF.Exp),
             reads=[TK(T_ARG)], writes=[TK(T_E)])
        tt("dve", v3(T_ARG), mv_b, th_b, ALU.mult, [K, "cF", TK(T_ARG)], [TK(T_ARG)])
        qi_s = A.alloc("qi_s", [128, 17 * 64], I32)
        range_reduce(tbv(T_T1), tbv(T_ARG), qi_s[:], tbv(T_T2), TK(T_ARG), TK(T_T1), "qi_s", TK(T_T2))
        sincos(tbv(T_S), tbv(T_C), tbv(T_T1), tbv(T_T2), TK(T_T1), TK(T_S), TK(T_C), TK(T_T2))
        tt("dve", tbv(T_ARE), tbv(T_E), tbv(T_C), ALU.mult, [TK(T_E), TK(T_C)], [TK(T_ARE)])
        tt("dve", tbv(T_AIM), tbv(T_E), tbv(T_S), ALU.mult, [TK(T_E), TK(T_S)], [TK(T_AIM)])
        ARE, AIM = TK(T_ARE), TK(T_AIM)
        are1 = tbv(T_ARE, 0, 1)
        aim1 = tbv(T_AIM, 0, 1)
        tt("dve", sm[:, s_t1, :], sm[:, s_lr, :], sm[:, s_lr, :], ALU.mult, [K], [K])
        tt("dve", sm[:, s_t2, :], pA[:, LIM:LIM + 64], pA[:, LIM:LIM + 64], ALU.mult, ["pA"], [K])
        tt("dve", sm[:, s_den, :], sm[:, s_t1, :], sm[:, s_t2, :], ALU.add, [K], [K])
        S.op("dve", lambda e: e.reciprocal(out=sm[:, s_den, :], in_=sm[:, s_den, :]), reads=[K], writes=[K])
        S.op("dve", lambda e: e.tensor_scalar(out=sm[:, s_t3, :], in0=are1, scalar1=-1.0, scalar2=None,
                                              op0=ALU.add), reads=[ARE], writes=[K])
        tt("dve", sm[:, s_t1, :], sm[:, s_t3, :], sm[:, s_lr, :], ALU.mult, [K], [K])
        tt("dve", sm[:, s_t2, :], aim1, pA[:, LIM:LIM + 64], ALU.mult, [AIM, "pA"], [K])
        tt("dve", sm[:, s_zr, :], sm[:, s_t1, :], sm[:, s_t2, :], ALU.add, [K], [K])
        tt("dve", sm[:, s_zr, :], sm[:, s_zr, :], sm[:, s_den, :], ALU.mult, [K], [K])
        tt("dve", sm[:, s_t1, :], aim1, sm[:, s_lr, :], ALU.mult, [AIM, K], [K])
        tt("dve", sm[:, s_t2, :], sm[:, s_t3, :], pA[:, LIM:LIM + 64], ALU.mult, [K, "pA"], [K])
        tt("dve", sm[:, s_zi, :], sm[:, s_t1, :], sm[:, s_t2, :], ALU.subtract, [K], [K])
        tt("dve", sm[:, s_zi, :], sm[:, s_zi, :], sm[:, s_den, :], ALU.mult, [K], [K])
        zr_b = mk(sm[:, s_zr, :], [[0, 8], [1, 64]])
        zi_b = mk(sm[:, s_zi, :], [[0, 8], [1, 64]])
        an_re = tsl(T_ARE)[:, 8 * 64:16 * 64].rearrange("p (m n) -> p m n", m=8)
        an_im = tsl(T_AIM)[:, 8 * 64:16 * 64].rearrange("p (m n) -> p m n", m=8)
        t1v = tsl(T_T1)[:, 0:512].rearrange("p (m n) -> p m n", m=8)
        t2v = tsl(T_T2)[:, 0:512].rearrange("p (m n) -> p m n", m=8)
        wr_v = wtab[:, 0, :].rearrange("p (m n) -> p m n", m=8)
        wi_v = wtab[:, 1, :].rearrange("p (m n) -> p m n", m=8)
        tt("dve", t1v, an_re, zr_b, ALU.mult, [ARE, K, TK(T_T1)], [TK(T_T1)])
        tt("dve", t2v, an_im, zi_b, ALU.mult, [AIM, K, TK(T_T2)], [TK(T_T2)])
        tt("dve", wr_v, t1v, t2v, ALU.subtract, [TK(T_T1), TK(T_T2)], ["wtab"])
        tt("dve", t1v, an_re, zi_b, ALU.mult, [ARE, K, TK(T_T1)], [TK(T_T1)])
        tt("dve", t2v, an_im, zr_b, ALU.mult, [AIM, K, TK(T_T2)], [TK(T_T2)])
        tt("dve", wi_v, t1v, t2v, ALU.add, [TK(T_T1), TK(T_T2), "wtab"], ["wtab"])
        S.op("act", lambda e: e.copy(out=sm[:, s_rho, :], in_=tbv(T_E, 16, 17)), reads=[TK(T_E)], writes=[K])
        S.op("act", lambda e: e.copy(out=sm[:, s_c8, :], in_=tbv(T_ARE, 16, 17)), reads=[ARE], writes=[K])
        S.op("act", lambda e: e.copy(out=sm[:, s_s8, :], in_=tbv(T_AIM, 16, 17)), reads=[AIM], writes=[K])
        S.op("act", lambda e: e.mul(out=sm[:, s_ns8, :], in_=tbv(T_AIM, 16, 17), mul=-1.0), reads=[AIM], writes=[K])
        S.op("dve", lambda e: e.tensor_scalar(out=sm[:, s_t1, :], in0=sm[:, s_th, :], scalar1=8.0, scalar2=2 * TWO_PI,
                                              op0=ALU.mult, op1=ALU.add), reads=[K], writes=[K])
        range_reduce(sm[:, s_phi, :], sm[:, s_t1, :], qi_s[:, 0:64], sm[:, s_t2, :], K, K, "qi_s", K)

        A.release(mt)
        dump("are", tsl(T_ARE), [128, 1088], TK(T_ARE))
        dump("aim", tsl(T_AIM), [128, 1088], TK(T_AIM))
        dump("sm", sm[:].rearrange("p a b -> p (a b)"), [128, 24 * 64], K)
        dump("wtab", wtab[:].rearrange("p a b -> p (a b)"), [128, 1024], "wtab")
        checkpoint("T")
        Wre = [A.alloc("Wre%d" % i, [128, 8, 128], BF16) for i in range(2)]
        Wim = [A.alloc("Wim%d" % i, [128, 8, 128], BF16) for i in range(2)]
        M1re = [A.alloc("M1re%d" % i, [128, 8, 128], BF16) for i in range(2)]
        M1im = [A.alloc("M1im%d" % i, [128, 8, 128], BF16) for i in range(2)]
        Mst = [A.alloc("Mst%d" % i, [128, 16, 256], BF16) for i in range(2)]
        _dm = A.alloc("Dm0", [128, 8, 3, 64], BF16)
        Dm = [_dm, _dm]
        tmpA = A.alloc("tmpA", [128, 512], F32)
        tmpB = A.alloc("tmpB", [128, 512], F32)
        U_bs = [A.alloc("U_b%d" % i, [128, 8, NCH], BF16) for i in range(2)]
        NB = 4 * 272
        SR, SI, CO, SN, T1, T2 = [A.alloc(nm, [128, NB], F32) for nm in ("SR", "SI", "CO", "SN", "T1", "T2")]
        Gre = A.alloc("Gre", [128, 8, 128], BF16)
        Gim = A.alloc("Gim", [128, 8, 128], BF16)
        Yb = A.alloc("Yb", [128, 8, 128], BF16)
        ytmp = T1[:, 0:NTOK]
        ytmp2 = T2[:, 0:NTOK]
        NPOS = (130, 258)

        def slotAP(tbl_i, base_slot, d, n0, rev):
            v = tsl(tbl_i)
            if not rev:
                return mk(v, [[1, 4], [64, 8], [0, 16]], off=base_slot * 64 + n0)
            return mk(v, [[1, 4], [-64, 8], [0, 16]], off=(base_slot + 7) * 64 + n0)

        def stage_P1(b, bf, dsel):
            KW = lambda nm, d: ("%s%d" % (nm, bf), d)
            for d in (dsel,):
                n0 = d * 32 + b * 4
                rev = (d == 1)
                cr = mk(pA[:, CR:CR + 1024], [[16, 4], [0, 8], [1, 16]], off=n0 * 16)
                ci = mk(pA[:, CI:CI + 1024], [[16, 4], [0, 8], [1, 16]], off=n0 * 16)
                br = mk(pA[:, BR:BR + 1024], [[16, 4], [0, 8], [1, 16]], off=n0 * 16)
                bi = mk(pA[:, BI:BI + 1024], [[16, 4], [0, 8], [1, 16]], off=n0 * 16)
                a_re = slotAP(T_ARE, 0, d, n0, rev)
                a_im = slotAP(T_AIM, 0, d, n0, rev)
                wv = wtab[:, 0, :]
                if not rev:
                    w_r = mk(wv, [[1, 4], [64, 8], [0, 16]], off=n0)
                    w_i = mk(wv, [[1, 4], [64, 8], [0, 16]], off=512 + n0)
                else:
                    w_r = mk(wv, [[1, 4], [-64, 8], [0, 16]], off=7 * 64 + n0)
                    w_i = mk(wv, [[1, 4], [-64, 8], [0, 16]], off=512 + 7 * 64 + n0)
                tA = tmpA[:, 0:512].rearrange("p (a b c) -> p a b c", a=4, b=8)
                tB = tmpB[:, 0:512].rearrange("p (a b c) -> p a b c", a=4, b=8)
                r4 = lambda t: t[bf][:, 4 * d:4 * d + 4, :].rearrange("p a (b c) -> p a b c", b=8)
                RD = ["pA", ARE, AIM, "wtab"]
                E_ = "dve"
                tt(E_, tA, cr, a_re, ALU.mult, RD + ["tmpA"], ["tmpA"])
                tt(E_, tB, ci, a_im, ALU.mult, RD + ["tmpB"], ["tmpB"])
                tt(E_, r4(M1re), tA, tB, ALU.subtract, ["tmpA", "tmpB"], [KW("M1re", d)])
                tt(E_, tA, cr, a_im, ALU.mult, RD + ["tmpA"], ["tmpA"])
                tt(E_, tB, ci, a_re, ALU.mult, RD + ["tmpB"], ["tmpB"])
                tt(E_, tA, tA, tB, ALU.add, ["tmpA", "tmpB"], ["tmpA"])
                S.op(E_, lambda e: e.tensor_scalar(out=r4(M1im), in0=tA, scalar1=-1.0, scalar2=None, op0=ALU.mult),
                     reads=["tmpA"], writes=[KW("M1im", d)])
                tt(E_, tA, br, w_r, ALU.mult, RD + ["tmpA"], ["tmpA"])
                tt(E_, tB, bi, w_i, ALU.mult, RD + ["tmpB"], ["tmpB"])
                tt(E_, r4(Wre), tA, tB, ALU.subtract, ["tmpA", "tmpB"], [KW("Wre", d)])
                tt(E_, tA, bi, w_r, ALU.mult, RD + ["tmpA"], ["tmpA"])
                tt(E_, tB, br, w_i, ALU.mult, RD + ["tmpB"], ["tmpB"])
                tt(E_, r4(Wim), tA, tB, ALU.add, ["tmpA", "tmpB"], [KW("Wim", d)])

        def stage_P2a(b, bf):
            U_b = U_bs[bf]
            UK = "U_b%d" % bf
            for nl in range(8):
                d, gq = nl // 4, nl % 4
                n = d * 32 + b * 4 + gq
                for (j3, col) in ((0, s_c8), (1, s_s8), (2, s_ns8)):
                    S.op("act", lambda e: e.activation(
                        out=Dm[bf][:, nl, j3, :], in_=I2, func=AF.Copy, scale=sm[:, col, n:n + 1]),
                        reads=[K, "cF"], writes=[("Dm0", nl)])
            ub = u_fm[:, b, :]
            for g8 in range(8):
                pb, pk = bank()
                for j in range(8):
                    rhs = ub[:, j * NCH:(j + 1) * NCH]
                    S.op("pe", lambda e: e.matmul(
                        pb[:, 0:NCH], lhsT=strips[:, g8, 16 * (7 - j):16 * (7 - j) + 128], rhs=rhs,
                        start=(j == 0), stop=(j == 7)),
                        reads=[("u_fm", b), "strips"], writes=[pk])
                S.op("act", lambda e: e.copy(out=U_b[:, g8, :], in_=pb[:, 0:NCH]),
                     reads=[pk], writes=[(UK, g8)])

        def stage_P2b(b, bf, dsel):
            KW = lambda nm, d: ("%s%d" % (nm, bf), d)
            for nl in range(4 * dsel, 4 * dsel + 4):
                d = nl // 4
                for hh in range(2):
                    r0, r1 = 64 * hh, 64 * hh + 64
                    pb, pk = bank()
                    wre, wim = Wre[bf][r0:r1, nl, :], Wim[bf][r0:r1, nl, :]
                    mm = [
                        (pb[:, 0:128], wre, M1re[bf][r0:r1, nl, :], True, False),
                        (pb[:, 0:128], wim, M1im[bf][r0:r1, nl, :], False, True),
                        (pb[:, 128:192], wre, Dm[bf][r0:r1, nl, 0, :], True, False),
                        (pb[:, 128:192], wim, Dm[bf][r0:r1, nl, 2, :], False, True),
                        (pb[:, 192:256], wre, Dm[bf][r0:r1, nl, 1, :], True, False),
                        (pb[:, 192:256], wim, Dm[bf][r0:r1, nl, 0, :], False, True),
                    ]
                    for (o, l, r, st, sp) in mm:
                        S.op("pe", lambda e: e.matmul(o, lhsT=l, rhs=r, start=st, stop=sp),
                             reads=[KW("Wre", d), KW("Wim", d), KW("M1re", d), KW("M1im", d), ("Dm0", nl)],
                             writes=[pk])
                    msk = maskA if d == 0 else maskB
                    slot = 2 * nl + hh
                    S.op("act", lambda e: e.copy(out=Mst[bf][:, slot, :], in_=pb[:, 0:256]),
                         reads=[pk], writes=[("Mst%d" % bf, slot)])
                    S.op("pool", lambda e: e.tensor_tensor(
                        out=Mst[bf][:, slot, 0:128], in0=Mst[bf][:, slot, 0:128], in1=msk, op=ALU.mult),
                        reads=[("Mst%d" % bf, slot), "cF"], writes=[("Mst%d" % bf, slot)])

        def stage_D(b, bf, d):
            U_b = U_bs[bf]
            UK = "U_b%d" % bf
            if True:
                NP = NPOS[d]
                W4 = 4 * NP
                n0 = d * 32 + b * 4
                v2 = lambda t: t[:, 0:W4].rearrange("p (a k) -> p a k", a=4)
                fl = lambda t: t[:, 0:W4]
                phib = mk(sm[:, s_phi, :], [[1, 4], [0, NP]], off=n0)
                kb = mk(kpos, [[0, 4], [1, NP]])
                tt("dve", v2(T1), phib, kb, ALU.mult, [K, "cF", "T1"], ["T1"])
                range_reduce(fl(T2), fl(T1), fl(SR).bitcast(I32), fl(SI), "T1", "T2", "SR", "SI")
                sincos(fl(SN), fl(CO), fl(T2), fl(SI), "T2", "SN", "CO", "SI")
                for gq in range(4):
                    pr, pkr = bank()
                    pi_, pki = bank()
                    nl = 4 * d + gq
                    for hh in range(2):
                        g8 = 4 * hh + gq
                        slot = 2 * nl + hh
                        uv = U_b[:, g8, :]
                        rhs = uv[:, 14:14 + NP] if d == 0 else mk(uv, [[-1, NP]], off=273)
                        S.op("pe", lambda e: e.matmul(
                            pr[64 * hh:64 * hh + 64, 0:NP], lhsT=Mst[bf][:, slot, 128:192], rhs=rhs, start=True, stop=True),
                            reads=[("Mst%d" % bf, slot), (UK, g8)], writes=[pkr])
                        S.op("pe", lambda e: e.matmul(
                            pi_[64 * hh:64 * hh + 64, 0:NP], lhsT=Mst[bf][:, slot, 192:256], rhs=rhs, start=True, stop=True),
                            reads=[("Mst%d" % bf, slot), (UK, g8)], writes=[pki])
                    S.op("act", lambda e: e.copy(out=SR[:, gq * NP:(gq + 1) * NP], in_=pr[:, 0:NP]),
                         reads=[pkr, "SR"], writes=["SR"])
                    S.op("act", lambda e: e.copy(out=SI[:, gq * NP:(gq + 1) * NP], in_=pi_[:, 0:NP]),
                         reads=[pki, "SI"], writes=["SI"])
                tt("dve", fl(T1), fl(SR), fl(SN), ALU.mult, ["SR", "SN", "T1"], ["T1"])
                tt("dve", fl(SR), fl(SR), fl(CO), ALU.mult, ["SR", "CO"], ["SR"])
                tt("dve", fl(T2), fl(SI), fl(SN), ALU.mult, ["SI", "SN", "T2"], ["T2"])
                tt("dve", fl(SI), fl(SI), fl(CO), ALU.mult, ["SI", "CO"], ["SI"])
                tt("dve", fl(SR), fl(SR), fl(T2), ALU.add, ["SR", "T2"], ["SR"])
                tt("dve", fl(SI), fl(SI), fl(T1), ALU.subtract, ["SI", "T1"], ["SI"])
                rhob = mk(sm[:, s_rho, :], [[1, 4], [0, NP]], off=n0)
                S.op("dve", lambda e: e.tensor_copy(out=v2(T1), in_=rhob), reads=[K, "T1"], writes=["T1"])
                S.op("dve", lambda e: e.memset(mk(T1[:], [[NP, 4], [1, 1]]), 0.0), reads=["T1"], writes=["T1"])
                S.op("dve", lambda e: e.tensor_tensor_scan(out=fl(T2), data0=fl(T1), data1=fl(SR), initial=0.0,
                                                           op0=ALU.mult, op1=ALU.add),
                     reads=["T1", "SR", "T2"], writes=["T2"])
                S.op("dve", lambda e: e.tensor_tensor_scan(out=fl(SR), data0=fl(T1), data1=fl(SI), initial=0.0,
                                                           op0=ALU.mult, op1=ALU.add),
                     reads=["T1", "SI", "SR"], writes=["SR"])
                k0 = 2 if d == 0 else 130
                sel = lambda t: mk(t[:], [[NP, 4], [1, 128]], off=k0 - 1)
                a3 = lambda t: t[:, 0:512].rearrange("p (a k) -> p a k", a=4)
                tt("dve", a3(SI), sel(T2), sel(CO), ALU.mult, ["T2", "CO", "SI"], ["SI"])
                tt("dve", a3(T1), sel(SR), sel(SN), ALU.mult, ["SR", "SN", "T1"], ["T1"])
                tt("dve", Gre[:, 4 * d:4 * d + 4, :], a3(SI), a3(T1), ALU.subtract, ["SI", "T1"], [("Gre", d)])
                tt("dve", a3(SI), sel(T2), sel(SN), ALU.mult, ["T2", "SN", "SI"], ["SI"])
                tt("dve", a3(T1), sel(SR), sel(CO), ALU.mult, ["SR", "CO", "T1"], ["T1"])
                tt("dve", Gim[:, 4 * d:4 * d + 4, :], a3(SI), a3(T1), ALU.add, ["SI", "T1"], [("Gim", d)])

        def stage_Y(b, bf):
            U_b = U_bs[bf]
            UK = "U_b%d" % bf
            for g8 in range(8):
                hh, gq = g8 // 4, g8 % 4
                r0, r1 = 64 * hh, 64 * hh + 64
                pb, pk = bank()
                first = True
                for d in range(2):
                    nl = 4 * d + gq
                    slot = 2 * nl + hh
                    u_own = U_b[:, g8, 16:144]
                    if d == 0:
                        gre, gim = Gre[r0:r1, nl, :], Gim[r0:r1, nl, :]
                    else:
                        gre = mk(Gre[r0:r1, nl, :], [[-1, 128]], off=127)
                        gim = mk(Gim[r0:r1, nl, :], [[-1, 128]], off=127)
                    seq = [(Mst[bf][:, slot, 0:128], u_own), (M1re[bf][r0:r1, nl, :], gre), (M1im[bf][r0:r1, nl, :], gim)]
                    for qi, (l, r) in enumerate(seq):
                        last = (d == 1 and qi == 2)
                        S.op("pe", lambda e: e.matmul(pb[:, 0:128], lhsT=l, rhs=r, start=first, stop=last),
                             reads=[("Mst%d" % bf, slot), (UK, g8), ("M1re%d" % bf, d), ("M1im%d" % bf, d),
                                    ("Gre", d), ("Gim", d)], writes=[pk])
                        first = False
                S.op("act", lambda e: e.copy(out=Yb[:, g8, :], in_=pb[:, 0:128]),
                     reads=[pk], writes=[("Yb", g8)])
            pbs = [(ps[6], ("ps", 6)), (ps[7], ("ps", 7))]
            for i in range(8):
                for g8 in range(8):
                    for half in range(2):
                        pb, pk = pbs[half]
                        o = mk(pb[:, 0:512], [[8, 64]], off=i)
                        S.op("pe", lambda e: e.matmul(
                            o, lhsT=strips[:, i, 16 * (7 - g8):16 * (7 - g8) + 128],
                            rhs=Yb[:, g8, 64 * half:64 * half + 64],
                            start=(i == 0 and g8 == 0), stop=(i == 7 and g8 == 7), skip_group_check=True),
                            reads=[("Yb", g8), "strips"], writes=[pk])
            return pbs

        def stage_E(b, pbs):
            dsk = vecs[:, V_DSK + b:V_DSK + b + 1]
            for half in range(2):
                pb, pk = pbs[half]
                c0 = 512 * half
                S.op("dve", lambda e: e.scalar_tensor_tensor(
                    out=ytmp[:, c0:c0 + 512].rearrange("p (c j) -> p c j", j=8),
                    in0=mk(u_fm[:, b, :], [[1, 64], [NCH, 8]], off=16 + 64 * half), scalar=dsk,
                    in1=pb[:, 0:512].rearrange("p (c j) -> p c j", j=8), op0=ALU.mult, op1=ALU.add),
                    reads=[pk, ("u_fm", b), "vecs", "T1"], writes=["T1"])
            if b == 0:
                dump("y0", ytmp, [128, NTOK], "T1")
                checkpoint("S0")
            S.op("act", lambda e: e.copy(out=gT[:, b, :], in_=ytmp), reads=["T1"], writes=[("gT", b)])

        stage_P1(0, 0, 0)
        stage_P1(0, 0, 1)
        stage_P2a(0, 0)
        stage_P2b(0, 0, 0)
        stage_P2b(0, 0, 1)
        for b in range(8):
            nb_ = (b + 1) % 2
            more = b < 7
            stage_D(b, b % 2, 0)
            if more:
                stage_P2a(b + 1, nb_)
                stage_P1(b + 1, nb_, 0)
            stage_D(b, b % 2, 1)
            if more:
                stage_P2b(b + 1, nb_, 0)
                stage_P1(b + 1, nb_, 1)
            pbs = stage_Y(b, b % 2)
            stage_E(b, pbs)
            if more:
                stage_P2b(b + 1, nb_, 1)
        print("S5 arena peak", A.peak - A.LO, "of", A.HI - A.LO)
        ps_pool[0] = list(range(8))

        checkpoint("S")
        dump("g", gT[:].rearrange("p a b -> p (a b)"), [128, 8 * NTOK], ("gT", 7), BF16)

        A.release(m1)
        g2 = gT
        G2 = [("gT", i) for i in range(8)]
        cv = A.alloc("cv", [128, 8, NTOK], BF16)
        mgd = A.alloc("mg", [128, 16, NTOK], BF16)
        m_mid = A.mark()

        wcv = [A.alloc("wcv%d" % i, [128, 3, 16, 128], BF16) for i in range(1)]
        _lo = A.addr["wcv0"]
        if _lo >= A.addr["pA"] and _lo + 12288 <= A.addr["tbK"] + 2 * 17 * 64 * 4:
            S.inherit["wcv0"] = S.all_events({"pA", "tbK"})
        else:
            print("note: wcv0 not on pA/tbK bytes, no early prefetch", _lo, A.addr["pA"], A.addr["tbK"])
        wcv += [A.alloc("wcv%d" % i, [128, 3, 16, 128], BF16) for i in range(1, 3)]
        TOPSLOT = 227776 - 16384
        wgl = A.fixed("wgl", [128, 8, 1024], BF16, TOPSLOT)
        xcs = A.alloc("xcs", [128, 1026], F32)
        zx = A.alloc("zx", [128, 1026], F32)
        vv = A.alloc("vv", [128, NTOK], F32)
        HT = [("hT", 0), ("hT", 1), ("hT", 2)]
        gtmp = [(A.alloc("gtmp%d" % i, [128, NTOK], F32)[:], "gtmp%d" % i) for i in range(4)]

        def gelu_b(b):
            tq, tk = gtmp[b % 4]
            S.op("act", lambda e: e.activation(out=tq, in_=gT[:, b, :], func=AF.Square), reads=[("gT", b), tk], writes=[tk])
            S.op("dve", lambda e: e.tensor_scalar(out=tq, in0=tq, scalar1=0.044715, scalar2=1.0,
                                                  op0=ALU.mult, op1=ALU.add), reads=[tk], writes=[tk])
            tt("dve", tq, tq, gT[:, b, :], ALU.mult, [tk, ("gT", b)], [tk])
            S.op("act", lambda e: e.activation(out=tq, in_=tq, func=AF.Sigmoid, scale=1.5957691216057308),
                 reads=[tk], writes=[tk])
            tt("dve", gT[:, b, :], tq, gT[:, b, :], ALU.mult, [tk, ("gT", b)], [("gT", b)])

        for j in range(8):
            wb = wcv[j % 3]
            wk = ("wcv%d" % (j % 3),)
            for s in range(3):
                c0 = 1024 + 1024 * s + 128 * j
                S.dma("pool", wb[:, s, :, :], w_in_v[:, :, c0:c0 + 128], "wcv%d" % (j % 3), writes=[wk])
            if j == 7:
                for kq in range(2):
                    S.dma("pool", wgl[:, 4 * kq:4 * kq + 4, :], w_glu_v[:, 4 * kq:4 * kq + 4, :], "wgl",
                          writes=[("wgl", kq)])


            banks = {}
            for s in range(3):
                for blk in range(2):
                    pb, pk = bank()
                    banks[(s, blk)] = (pb, pk)
                    for k in range(16):
                        S.op("pe", lambda e, pb=pb, s=s, k=k, blk=blk, wb=wb: e.matmul(
                            pb[:, 0:512], lhsT=wb[:, s, k, :], rhs=hT[:, k, 512 * blk:512 * blk + 512],
                            start=(k == 0), stop=(k == 15)), reads=[wk] + HT, writes=[pk])
            pbh, pkh = bank()
            for si, s in enumerate((0, 2)):
                for k in range(16):
                    S.op("pe", lambda e, si=si, s=s, k=k, wb=wb: e.matmul(
                        pbh[:, 2 * si:2 * si + 2], lhsT=wb[:, s, k, :], rhs=hT[:, k, 1024:1026],
                        start=(k == 0), stop=(k == 15)), reads=[wk] + HT, writes=[pkh])
            for blk in range(2):
                pb, pk = banks[(0, blk)]
                S.op("act", lambda e, pb=pb, blk=blk: e.copy(out=xcs[:, 1 + 512 * blk:513 + 512 * blk], in_=pb[:, 0:512]),
                     reads=[pk], writes=[("xcs", blk)])
            S.op("act", lambda e: e.copy(out=xcs[:, 0:1], in_=pbh[:, 0:1]), reads=[pkh], writes=[("xcs", 2)])
            S.op("act", lambda e: e.copy(out=xcs[:, 1025:1026], in_=pbh[:, 1:2]), reads=[pkh], writes=[("xcs", 3)])
            for blk in range(2):
                pb, pk = banks[(2, blk)]
                S.op("dve", lambda e, pb=pb, blk=blk: e.tensor_tensor(
                    out=zx[:, 1 + 512 * blk:513 + 512 * blk], in0=pb[:, 0:512], in1=xcs[:, 1 + 512 * blk:513 + 512 * blk],
                    op=ALU.mult), reads=[pk, ("xcs", blk)], writes=[("zx", blk)])
            S.op("dve", lambda e: e.tensor_tensor(out=zx[:, 0:1], in0=pbh[:, 2:3], in1=xcs[:, 0:1], op=ALU.mult),
                 reads=[pkh, ("xcs", 2)], writes=[("zx", 2)])
            S.op("dve", lambda e: e.tensor_tensor(out=zx[:, 1025:1026], in0=pbh[:, 3:4], in1=xcs[:, 1025:1026], op=ALU.mult),
                 reads=[pkh, ("xcs", 3)], writes=[("zx", 3)])
            ZK = [("zx", i) for i in range(4)]
            w0 = vecs[:, V_CW + j:V_CW + j + 1]
            w1 = vecs[:, V_CW + 8 + j:V_CW + 8 + j + 1]
            w2 = vecs[:, V_CW + 16 + j:V_CW + 16 + j + 1]
            cb = vecs[:, V_CB + j:V_CB + j + 1]
            S.op("act", lambda e: e.activation(out=vv[:], in_=zx[:, 1:1025], func=AF.Identity, bias=cb, scale=w1),
                 reads=ZK + ["vecs", "vv"], writes=["vv"])
            S.op("dve", lambda e: e.scalar_tensor_tensor(out=vv[:], in0=zx[:, 0:1024], scalar=w0, in1=vv[:],
                                                         op0=ALU.mult, op1=ALU.add), reads=ZK + ["vv", "vecs"], writes=["vv"])
            S.op("dve", lambda e: e.scalar_tensor_tensor(out=vv[:], in0=zx[:, 2:1026], scalar=w2, in1=vv[:],
                                                         op0=ALU.mult, op1=ALU.add), reads=ZK + ["vv", "vecs"], writes=["vv"])
            for blk in range(2):
                pb, pk = banks[(1, blk)]
                S.op("dve", lambda e, pb=pb, blk=blk, j=j: e.tensor_tensor(
                    out=cv[:, j, 512 * blk:512 * blk + 512], in0=pb[:, 0:512], in1=vv[:, 512 * blk:512 * blk + 512],
                    op=ALU.mult), reads=[pk, "vv"], writes=[("cv", j)])
            gelu_b(j)
            if j == 5:
                pass
        checkpoint("C0")
        dump("cv", cv[:].rearrange("p a b -> p (a b)"), [128, 8 * NTOK], ("cv", 7), BF16)
        A.release(m_mid)

        g3 = A.alloc("g3", [128, 8, NTOK], BF16)
        sg = A.alloc("sg", [128, 2, 512], F32)
        for m in range(8):
            for blk in range(2):
                pb, pk = bank()
                for k in range(8):
                    S.op("pe", lambda e, pb=pb, k=k, m=m, blk=blk: e.matmul(
                        pb[:, 0:512], lhsT=wgl[:, k, 128 * m:128 * m + 128], rhs=g2[:, k, 512 * blk:512 * blk + 512],
                        start=(k == 0), stop=(k == 7)), reads=[("wgl", 0), ("wgl", 1)] + G2, writes=[pk])
                S.op("act", lambda e, pb=pb, blk=blk, m=m: e.activation(
                    out=sg[:, blk, :], in_=pb[:, 0:512], func=AF.Sigmoid, bias=vecs[:, V_BGLU + m:V_BGLU + m + 1]),
                    reads=[pk, "vecs", ("sg", blk)], writes=[("sg", blk)])
                S.op("dve", lambda e, blk=blk, m=m: e.tensor_tensor(
                    out=g3[:, m, 512 * blk:512 * blk + 512], in0=sg[:, blk, :], in1=g2[:, m, 512 * blk:512 * blk + 512],
                    op=ALU.mult), reads=[("sg", blk)] + G2, writes=[("g3", m)])
        G3 = [("g3", m) for m in range(8)]
        CV = [("cv", m) for m in range(8)]

        wmg = [A.alloc("wmg%d" % i, [128, 48, 256], BF16) for i in range(2)]
        et = A.alloc("et", [128, 2, 4, 512], F32)
        A.retire_fixed(["wgl"], TOPSLOT, TOPSLOT + 16384)
        wo0 = A.fixed("wo0", [128, 16, 512], BF16, TOPSLOT)
        for m in range(16):
            mp, mi = m // 2, m % 2
            if m == 12:
                for kq in range(4):
                    S.dma("pool", wo0[:, 4 * kq:4 * kq + 4, :], w_o_v[:, 4 * kq:4 * kq + 4, 0:512], "wo0", writes=[("wo0",)])
            wk = ("wmg%d" % (mp % 2),)
            slot = "wmg%d" % (mp % 2)
            if mi == 0:
                wb2 = wmg[mp % 2]
                c0 = 256 * mp
                S.dma("pool", wb2[:, 0:8, :], w_s_v[:, :, c0:c0 + 256], slot, writes=[wk])
                S.dma("pool", wb2[:, 8:16, :], w_c_v[:, :, c0:c0 + 256], slot, writes=[wk])
                S.dma("pool", wb2[:, 16:32, :], w_in_v[:, :, 4096 + c0:4096 + c0 + 256], slot, writes=[wk])
                S.dma("pool", wb2[:, 32:48, :], w_in_v[:, :, 6144 + c0:6144 + c0 + 256], slot, writes=[wk])
            wb = wb2[:, :, 128 * mi:128 * mi + 128]
            for blk in range(2):
                cs = slice(512 * blk, 512 * blk + 512)
                jobs = [(0, 8, g3, G3), (8, 8, cv, CV), (16, 16, hT, HT), (32, 16, hT, HT)]
                pbk = []
                for (w0_, nk, src, skeys) in jobs:
                    pb, pk = bank()
                    pbk.append((pb, pk))
                    for k in range(nk):
                        S.op("pe", lambda e, pb=pb, k=k, w0_=w0_, src=src, nk=nk, cs=cs, wb=wb: e.matmul(
                            pb[:, 0:512], lhsT=wb[:, w0_ + k, :], rhs=src[:, k, cs], start=(k == 0), stop=(k == nk - 1)),
                            reads=[wk] + skeys, writes=[pk])
                ek = lambda i: ("et", blk, i)
                (pys, kys), (pyc, kyc), (pgs, kgs), (pgc, kgc) = pbk
                S.op("act", lambda e, pys=pys, blk=blk: e.copy(out=et[:, blk, 0, :], in_=pys[:, 0:512]),
                     reads=[kys, ek(0)], writes=[ek(0)])
                S.op("act", lambda e, pyc=pyc, blk=blk: e.copy(out=et[:, blk, 1, :], in_=pyc[:, 0:512]),
                     reads=[kyc, ek(1)], writes=[ek(1)])
                S.op("act", lambda e, pgs=pgs, blk=blk, m=m: e.activation(
                    out=et[:, blk, 2, :], in_=pgs[:, 0:512], func=AF.Sigmoid, bias=vecs[:, V_GB + m:V_GB + m + 1]),
                    reads=[kgs, ek(2), "vecs"], writes=[ek(2)])
                S.op("act", lambda e, pgc=pgc, blk=blk, m=m: e.activation(
                    out=et[:, blk, 3, :], in_=pgc[:, 0:512], func=AF.Sigmoid, bias=vecs[:, V_GB + 16 + m:V_GB + 16 + m + 1]),
                    reads=[kgc, ek(3), "vecs"], writes=[ek(3)])
                tt("dve", et[:, blk, 0, :], et[:, blk, 0, :], et[:, blk, 2, :], ALU.mult, [ek(0), ek(2)], [ek(0)])
                tt("dve", et[:, blk, 1, :], et[:, blk, 1, :], et[:, blk, 3, :], ALU.mult, [ek(1), ek(3)], [ek(1)])
                tt("dve", mgd[:, m, cs], et[:, blk, 0, :], et[:, blk, 1, :], ALU.add, [ek(0), ek(1)], [("mg", m)])
        MG = [("mg", m) for m in range(16)]
        checkpoint("E0")
        dump("mg", mgd[:].rearrange("p a b -> p (a b)"), [128, 16 * NTOK], ("mg", 15), BF16)
        A.release(m_mid)

        osb = A.alloc("osb", [128, 8, D], F32)
        ssq = A.alloc("ssq", [128, 8, 4], F32)
        mF = A.mark()
        wo = [wo0, A.alloc("wo1", [128, 16, 512], BF16)]
        sqj = A.alloc("sqj", [128, 512], BF16)
        for nb in range(4):
            wb = wo[nb % 2]
            wk = ("wo%d" % (nb % 2),)
            for kq in range(4):
                if nb == 0:
                    break
                S.dma("pool", wb[:, 4 * kq:4 * kq + 4, :], w_o_v[:, 4 * kq:4 * kq + 4, 512 * nb:512 * nb + 512],
                      "wo%d" % (nb % 2), writes=[wk])
            for t in range(8):
                pb, pk = bank()
                for k in range(16):
                    S.op("pe", lambda e, pb=pb, k=k, t=t, wb=wb: e.matmul(
                        pb[:, 0:512], lhsT=mgd[:, k, 128 * t:128 * t + 128], rhs=wb[:, k, :],
                        start=(k == 0), stop=(k == 15)), reads=[wk] + MG, writes=[pk])
                S.op("act", lambda e, pb=pb, t=t, nb=nb: e.activation(
                    out=sqj[:], in_=pb[:, 0:512], func=AF.Square, accum_out=ssq[:, t, nb:nb + 1]),
                    reads=[pk, "sqj"], writes=["sqj", ("ssq", t)])
                S.op("dve", lambda e, pb=pb, t=t, nb=nb: e.tensor_copy(out=osb[:, t, 512 * nb:512 * nb + 512], in_=pb[:, 0:512]),
                     reads=[pk], writes=[("osb", t)])
        checkpoint("F")
        A.retire_fixed(["wo0"], TOPSLOT, TOPSLOT + 16384)
        A.release(mF)
        gpost = A.alloc("gpost", [128, 2, D], F32)
        S.dma("sp", gpost[:].rearrange("p a b -> p (a b)"), gpost_d, "gpost", writes=["gpost"])
        xin2 = [(A.alloc("xr0", [128, D], F32), ("xr0",)), (A.overlay("xr1", [128, D], F32, "gT", 0), ("xr1",))]
        xin3 = [(A.alloc("xs0", [128, D], F32), ("xs0",)), (A.overlay("xs1", [128, D], F32, "gT", 8192), ("xs1",))]
        sq2 = (A.alloc("sq2", [128, D], BF16), ("sq2",))
        sm2 = A.alloc("sm2", [128, 32], F32)

        def post_norm_residual(t, src, srck, ssum_ap, ssk, gi, res_d, res_rows, dstk):
            xb, xk = xin2[t % 2]
            S.dma("sp", xb[:], res_d[res_rows:res_rows + 128, :], xk[0], writes=[xk])
            sc = sm2[:, 2 * (t % 8):2 * (t % 8) + 1]
            sk = ("sm2", t % 8)
            S.op("dve", lambda e: e.tensor_reduce(out=sc, in_=ssum_ap, axis=mybir.AxisListType.X, op=ALU.add),
                 reads=[ssk], writes=[sk])
            S.op("act", lambda e: e.activation(out=sc, in_=sc, func=AF.Sqrt, bias=1e-6, scale=1.0 / D), reads=[sk], writes=[sk])
            S.op("dve", lambda e: e.reciprocal(out=sc, in_=sc), reads=[sk], writes=[sk])
            S.op("act", lambda e: e.activation(out=src, in_=src, func=AF.Copy, scale=sc), reads=[srck, sk], writes=[srck])
            tt("dve", src, src, gpost[:, gi, :], ALU.mult, [srck, "gpost"], [srck])
            tt("dve", src, src, xb[:], ALU.add, [srck, xk], [srck])

        sm4 = A.alloc("sm4", [128, 32], F32)
        for t in range(8):
            sc = sm2[:, t:t + 1]
            sk = ("sm2", t)
            S.op("dve", lambda e: e.tensor_reduce(out=sc, in_=ssq[:, t, :], axis=mybir.AxisListType.X, op=ALU.add),
                 reads=[("ssq", t)], writes=[sk])
            S.op("act", lambda e: e.activation(out=sc, in_=sc, func=AF.Sqrt, bias=1e-6, scale=1.0 / D), reads=[sk], writes=[sk])
            S.op("dve", lambda e: e.reciprocal(out=sc, in_=sc), reads=[sk], writes=[sk])

        def g_loadx(t):
            xb, xk = xin2[t % 2]
            S.dma("sp", xb[:], xseq[128 + 128 * t:256 + 128 * t, :], xk[0], writes=[xk])

        g_loadx(0)
        g_loadx(1)

        def g_B1(t):
            src, srck = osb[:, t, :], ("osb", t)
            xb, xk = xin2[t % 2]
            S.op("dve", lambda e: e.scalar_tensor_tensor(out=src, in0=src, scalar=sm2[:, t:t + 1], in1=gpost[:, 0, :],
                                                         op0=ALU.mult, op1=ALU.mult),
                 reads=[srck, ("sm2", t), "gpost"], writes=[srck])
            tt("dve", src, src, xb[:], ALU.add, [srck, xk], [srck])
            if t + 2 < 8:
                g_loadx(t + 2)
            S.dma("sp", out_d[128 * t:128 * t + 128, :], src, "hs_scr", reads=[srck], writes=[("hs_scr", t)])
            if t == 0:
                dump("hs2", osb[:, 0, :], [128, D], ("osb", 0))
            sc = sm4[:, t:t + 1]
            S.op("act", lambda e: e.activation(out=sq2[0][:], in_=src, func=AF.Square, accum_out=sc),
                 reads=[srck], writes=[sq2[1], ("sm4", t)])
            S.op("act", lambda e: e.activation(out=sc, in_=sc, func=AF.Sqrt, bias=1e-6, scale=1.0 / D),
                 reads=[("sm4", t)], writes=[("sm4", t)])

        def g_B2(t):
            src, srck = osb[:, t, :], ("osb", t)
            sc = sm4[:, t:t + 1]
            xs, xsk = xin3[t % 2]
            S.op("dve", lambda e: e.reciprocal(out=sc, in_=sc), reads=[("sm4", t)], writes=[("sm4", t)])
            S.op("act", lambda e: e.activation(out=xs[:], in_=src, func=AF.Copy, scale=sc),
                 reads=[srck, ("sm4", t)], writes=[xsk])

        def g_C(t):
            xs, xsk = xin3[t % 2]
            for q in range(4):
                pb, pk = bank()
                for kk in range(4):
                    k = 4 * q + kk
                    S.op("pe", lambda e: e.transpose(
                        out=pb[:, kk * 128:(kk + 1) * 128], in_=xs[:, k * 128:(k + 1) * 128], identity=ident),
                        reads=[xsk, "cF"], writes=[pk])
                g_b = mk(vecs[:, V_GFPRE + 4 * q:V_GFPRE + 4 * q + 4], [[1, 4], [0, 128]])
                S.op("dve", lambda e: e.tensor_tensor(
                    out=hT[:, 4 * q:4 * q + 4, 128 * t:128 * t + 128],
                    in0=pb[:].rearrange("p (a b) -> p a b", a=4), in1=g_b, op=ALU.mult),
                    reads=[pk, "vecs"], writes=[("hT", t // 4)])

        for it in range(10):
            if it < 8:
                g_B1(it)
            if 1 <= it < 9:
                g_B2(it - 1)
            if it >= 2:
                g_C(it - 2)

        checkpoint("G")
        A.release(m_main)
        facc = A.alloc("facc", [128, 8, D], F32)
        agr = A.alloc("agr", [128, 11, NTOK], BF16)
        WB = A.alloc("WB", [128, 4 * 5632], BF16)
        wfo4 = [WB[:, 5632 * i:5632 * (i + 1)].rearrange("p (k n) -> p k n", k=11) for i in range(4)]
        wfo = wfo4[0:2]
        wfi = [WB[:, 11264 + 4096 * i:11264 + 4096 * (i + 1)].rearrange("p (s k n) -> p s k n", s=2, k=16) for i in range(2)]
        for nm_ in ("wfi0", "wfi1", "wfo0", "wfo1", "wfoL2", "wfoL3"):
            S.inherit[nm_] = S.inherit.get("WB", [])
        sgt = A.alloc("sgt", [128, 2, 512], F32)
        gpost = A.alloc("gpostb", [128, 2, D // 2], F32)
        gpost_f = gpost[:].rearrange("p a b -> p (a b)")
        S.dma("sp", gpost_f, gpost_d[:, D:2 * D], "gpostb", writes=["gpostb"])
        wfi.append(A.alloc("wfi2", [128, 2, 16, 128], BF16)[:])
        xin2 = [A.alloc("xq%d" % i, [128, D], F32) for i in range(2)]
        ssq2 = A.alloc("ssq2", [128, 8, 4], F32)
        sqj2 = A.alloc("sqj2", [128, 512], BF16)
        sm3 = A.alloc("sm3", [128, 32], F32)
        wfi_ctr = 0
        wfo_ctr = 0

        def final_stats(t):
            sc = sm3[:, 2 * t:2 * t + 1]
            sk = ("sm3", t)
            xb = xin2[t % 2]
            xk = ("xq%d" % (t % 2),)
            S.dma("sp", xb[:], out_d[128 * t:128 * t + 128, :], "xq%d" % (t % 2), reads=[("hs_scr", t)], writes=[xk])
            S.op("dve", lambda e: e.tensor_reduce(out=sc, in_=ssq2[:, t, :], axis=mybir.AxisListType.X, op=ALU.add),
                 reads=[("ssq2", t, nb) for nb in range(4)], writes=[sk])
            S.op("act", lambda e: e.activation(out=sc, in_=sc, func=AF.Sqrt, bias=1e-6, scale=1.0 / D), reads=[sk], writes=[sk])
            S.op("dve", lambda e: e.reciprocal(out=sc, in_=sc), reads=[sk], writes=[sk])

        def final_heavy(t):
            src = facc[:, t, :]
            fks = [("facc", t, nb) for nb in range(4)]
            xb = xin2[t % 2]
            xk = ("xq%d" % (t % 2),)
            ok = ("fo", t)
            S.op("dve", lambda e: e.scalar_tensor_tensor(out=src, in0=src, scalar=sm3[:, 2 * t:2 * t + 1], in1=gpost_f,
                                                         op0=ALU.mult, op1=ALU.mult),
                 reads=fks + [("sm3", t), "gpostb"], writes=[ok])
            tt("dve", src, src, xb[:], ALU.add, [ok, xk], [ok])
            S.dma("sp", out_d[128 * t:128 * t + 128, :], src, "outst", reads=[ok, xk], writes=[("hs_scr", t)], is_output=True)

        for grp in range(4):
            for mi in range(11):
                mch = 11 * grp + mi
                wb = wfi[wfi_ctr % 3]
                wk = ("wfi%d" % (wfi_ctr % 3),)
                slot = "wfi%d" % (wfi_ctr % 3)
                wfi_ctr += 1
                S.dma("pool", wb[:, 0, :, :], w_fi_v[:, :, 128 * mch:128 * mch + 128], slot, writes=[wk])
                S.dma("pool", wb[:, 1, :, :], w_fi_v[:, :, FFN + 128 * mch:FFN + 128 * mch + 128], slot, writes=[wk])
                for blk in range(2):
                    cs = slice(512 * blk, 512 * blk + 512)
                    pg, kg = bank()
                    pu, ku = bank()
                    for (pb, pk, s) in ((pg, kg, 0), (pu, ku, 1)):
                        for k in range(16):
                            S.op("pe", lambda e, pb=pb, s=s, k=k, cs=cs, wb=wb: e.matmul(
                                pb[:, 0:512], lhsT=wb[:, s, k, :], rhs=hT[:, k, cs], start=(k == 0), stop=(k == 15)),
                                reads=[wk, ("hT", 0), ("hT", 1)], writes=[pk])
                    S.op("act", lambda e, pg=pg, blk=blk: e.activation(out=sgt[:, blk, :], in_=pg[:, 0:512], func=AF.Silu),
                         reads=[kg, ("sgt", blk)], writes=[("sgt", blk)])
                    S.op("dve", lambda e, pu=pu, blk=blk, mi=mi, cs=cs: e.tensor_tensor(
                        out=agr[:, mi, cs], in0=pu[:, 0:512], in1=sgt[:, blk, :], op=ALU.mult),
                        reads=[ku, ("sgt", blk), ("agr", mi)], writes=[("agr", mi)])
            AG = [("agr", i) for i in range(11)]

            def fo_tile(t, nb, wb, wk):
                pb, pk = bank()
                for k in range(11):
                    S.op("pe", lambda e: e.matmul(
                        pb[:, 0:512], lhsT=agr[:, k, 128 * t:128 * t + 128], rhs=wb[:, k, :],
                        start=(k == 0), stop=(k == 10)), reads=[wk] + AG, writes=[pk])
                dst = facc[:, t, 512 * nb:512 * nb + 512]
                fk = ("facc", t, nb)
                if grp == 0:
                    S.op("act", lambda e: e.copy(out=dst, in_=pb[:, 0:512]), reads=[pk], writes=[fk])
                else:
                    S.op("dve", lambda e: e.tensor_tensor(out=dst, in0=dst, in1=pb[:, 0:512], op=ALU.add),
                         reads=[pk, fk], writes=[fk])
                if grp == 3:
                    S.op("act", lambda e: e.activation(
                        out=sqj2[:], in_=dst, func=AF.Square, accum_out=ssq2[:, t, nb:nb + 1]),
                        reads=[fk, "sqj2"], writes=["sqj2", ("ssq2", t, nb)])

            if grp < 3:
                for nb in range(4):
                    wb = wfo[wfo_ctr % 2]
                    wk = ("wfo%d" % (wfo_ctr % 2),)
                    slot = "wfo%d" % (wfo_ctr % 2)
                    wfo_ctr += 1
                    S.dma("pool", wb[:, :, :], w_fo_v[:, 11 * grp:11 * grp + 11, 512 * nb:512 * nb + 512], slot, writes=[wk])
                    for t in range(8):
                        fo_tile(t, nb, wb, wk)
            else:
                wks = [("wfo0",), ("wfo1",), ("wfoL2",), ("wfoL3",)]
                extra = [[], [], [("wfi0",), ("wfi1",)], [("wfi1",)]]
                for nb in range(4):
                    S.dma("pool", wfo4[nb][:, :, :], w_fo_v[:, 11 * grp:11 * grp + 11, 512 * nb:512 * nb + 512],
                          wks[nb][0], writes=[wks[nb]] + extra[nb])
                for t in range(8):
                    for nb in range(4):
                        fo_tile(t, nb, wfo4[nb], wks[nb])
                    final_stats(t)
                    if t > 0:
                        final_heavy(t - 1)
                final_heavy(7)

def _consts():
    cF = np.zeros((128, 768), np.float32)
    cF[:, 0:128] = np.eye(128, dtype=np.float32)
    r = np.arange(128)
    jj, ii = r[:, None] // 16, r[None, :] // 16
    cF[:, 128:256] = (ii >= jj).astype(np.float32)
    cF[:, 256:384] = (jj >= ii).astype(np.float32)
    cF[:, 384:448] = np.concatenate([np.eye(64), np.eye(64)], 0).astype(np.float32)
    cF[:, 448:465] = np.array([1, 2, 3, 4, 5, 6, 7, 8, -1, -2, -3, -4, -5, -6, -7, -8, 8], np.float32)[None, :]
    cF[:, 480:768] = np.arange(288, dtype=np.float32)[None, :]
    strips = np.zeros((128, 8, 240), np.float32)
    for x in range(8):
        for c in range(16):
            strips[16 * x + c, x, 7 * 16 + c] = 1.0
    return cF, strips.reshape(128, 1920).astype(ml_dtypes.bfloat16)


def _core_inputs(inp, b, half, shared):
    x = np.asarray(inp["x"])[b]
    meta = np.asarray(inp["meta"])
    z112 = np.zeros((112, D), np.float32)
    z128 = np.zeros((128, D), np.float32)
    if half == 0:
        xseq = np.concatenate([z112, meta, x[:1024], x[1024:], z128], 0)
        od = (0, 1)
    else:
        xseq = np.concatenate([z128, x[1024:][::-1], x[:1024][::-1], meta[::-1], z112], 0)
        od = (1, 0)
    pA = np.zeros((128, 4288), np.float32)
    lam_re, lam_im, log_dt = inp["lam_re"][0], inp["lam_im"][0], inp["log_dt"][0]
    b_re, b_im, c_re, c_im = inp["b_re"][0], inp["b_im"][0], inp["c_re"][0], inp["c_im"][0]
    for d in range(2):
        o = od[d]
        for b8 in range(8):
            for gq in range(4):
                n = d * 32 + b8 * 4 + gq
                for hh in range(2):
                    g = 8 * b8 + 4 * hh + gq
                    rs = slice(64 * hh, 64 * hh + 64)
                    pA[rs, 0 + n] = lam_re[o, g]
                    pA[rs, 64 + n] = lam_im[o, g]
                    pA[rs, 128 + n] = log_dt[o, g]
                    pA[rs, 192 + 16 * n:192 + 16 * n + 16] = b_re[o, g]
                    pA[rs, 1216 + 16 * n:1216 + 16 * n + 16] = b_im[o, g]
                    pA[rs, 2240 + 16 * n:2240 + 16 * n + 16] = c_re[o, g].T
                    pA[rs, 3264 + 16 * n:3264 + 16 * n + 16] = c_im[o, g].T
    vecs = np.zeros((128, 128), np.float32)

    def fm(v):
        return np.asarray(v, np.float32).reshape(-1, 128).T

    vecs[:, 0:16] = fm(inp["g_mix_pre"][0])
    vecs[:, 16:32] = fm(inp["g_ffn_pre"][0])
    vecs[:, 32:64] = fm(inp["gate_b"][0])
    vecs[:, 64:72] = fm(inp["d_skip"][0])
    vecs[:, 72:80] = fm(inp["b_glu"][0])
    cw = np.asarray(inp["conv_w"][0])
    if half == 1:
        cw = cw[::-1]
    vecs[:, 80:88] = fm(cw[0])
    vecs[:, 88:96] = fm(cw[1])
    vecs[:, 96:104] = fm(cw[2])
    vecs[:, 104:112] = fm(inp["conv_b"][0])
    m = dict(shared)
    m.update({"xseq": np.ascontiguousarray(xseq), "pA": pA, "vecs": vecs})
    return m


_NC_CACHE = {}


def kernel(**inputs):
    inp = {k: np.asarray(v) for k, v in inputs.items()}
    cF, strips = _consts()
    gpost = np.concatenate([np.broadcast_to(inp["g_mix_post"][0][None, :], (128, D)),
                            np.broadcast_to(inp["g_ffn_post"][0][None, :], (128, D))], 1).astype(np.float32)
    shared = {
        "gpost": np.ascontiguousarray(gpost), "cF": cF, "strips": strips,
        "w_in": np.ascontiguousarray(inp["w_in"][0]), "w_glu": np.ascontiguousarray(inp["w_glu"][0]),
        "w_s_up": np.ascontiguousarray(inp["w_s_up"][0]), "w_c_up": np.ascontiguousarray(inp["w_c_up"][0]),
        "w_o": np.ascontiguousarray(inp["w_o"][0]), "w_ffn_in": np.ascontiguousarray(inp["w_ffn_in"][0]),
        "w_ffn_out": np.ascontiguousarray(inp["w_ffn_out"][0]),
    }
    in_maps = [_core_inputs(inp, c // 2, c % 2, shared) for c in range(8)]
    if "nc" not in _NC_CACHE:
        _NC_CACHE["nc"] = build_nc(DEBUG)
    nc, dbg = _NC_CACHE["nc"]
    ncores = NCORES
    res = run_bass_kernel_spmd(nc, in_maps[:ncores], core_ids=list(range(ncores)))
    out = np.zeros((4, 2048, D), np.float32)
    for c in range(ncores):
        o = np.asarray(res.results[c]["out"])
        if c % 2 == 0:
            out[c // 2, :1024] = o
        else:
            out[c // 2, 1024:] = o[::-1]
    if DEBUG:
        kernel.last_results = res.results
    return out
```
